# Optimizing a Trainium2 kernel written in Bass

```python
import jax, jax.numpy as jnp
from jax import lax
import numpy as np

D_MODEL = 2048
BATCH = 4
SEQ = 4096
DEPTH = 1

PLE_DIM = 256
D_MIX = D_MODEL
GLA_WIDTH = D_MIX // 2
CONV_WIDTH = D_MIX - GLA_WIDTH
GLA_HEADS = 4
GLA_DV = GLA_WIDTH // GLA_HEADS
GLA_DK = GLA_DV // 2
GLA_KEY_WIDTH = GLA_HEADS * GLA_DK
GATE_RANK = 16
GATE_TAU = 16.0
CHUNK = 64
CONV_K = 31
CONV_GROUPS = 8
CONV_GROUP_WIDTH = CONV_WIDTH // CONV_GROUPS
EPS = 1e-6

Q_END = GLA_KEY_WIDTH
K_END = Q_END + GLA_KEY_WIDTH
V_END = K_END + GLA_WIDTH
ALOW_END = V_END + GATE_RANK
GA_END = ALOW_END + GLA_WIDTH
CVAL_END = GA_END + CONV_WIDTH
CGATE_END = CVAL_END + CONV_WIDTH
D_IN = CGATE_END + CONV_WIDTH

kernel_name = "hybrid_gla_conformer_ple"


def rmsnorm(x, g):
    xf = x.astype(jnp.float32)
    y = xf * lax.rsqrt(jnp.mean(xf * xf, axis=-1, keepdims=True) + EPS)
    return (y * g.astype(jnp.float32)).astype(x.dtype)


def gla_chunked(q, k, v, log_a):
    B, H, S, DK = q.shape
    DV = v.shape[-1]
    N = S // CHUNK

    def to_chunks(t):
        return jnp.moveaxis(t.reshape(B, H, N, CHUNK, t.shape[-1]), 2, 0)

    qc, kc, vc, gc = to_chunks(q), to_chunks(k), to_chunks(v), to_chunks(log_a)
    bc = jnp.cumsum(gc, axis=-2)
    causal = jnp.tril(jnp.ones((CHUNK, CHUNK), dtype=bool))[:, :, None]

    def step(state, inp):
        qi, ki, vi, bi = inp
        diff = bi[..., :, None, :] - bi[..., None, :, :]
        decay = jnp.exp(jnp.where(causal, diff, -jnp.inf))
        scores = jnp.einsum('bhid,bhjd,bhijd->bhij', qi, ki, decay)
        o = (jnp.einsum('bhij,bhjv->bhiv', scores, vi)
             + jnp.einsum('bhid,bhdv->bhiv', qi * jnp.exp(bi), state))
        b_last = bi[..., -1:, :]
        state = (state * jnp.exp(b_last)[..., 0, :, None]
                 + jnp.einsum('bhjd,bhjv->bhdv', ki * jnp.exp(b_last - bi), vi))
        return state, o

    state0 = jnp.zeros((B, H, DK, DV), jnp.float32)
    _, oc = lax.scan(step, state0, (qc, kc, vc, bc))
    return jnp.moveaxis(oc, 0, 2).reshape(B, H, S, DV)


def hybrid_layer(h, p_i, norm_mix, w_in, w_alpha, b_alpha, gla_norm, conv_w, conv_b,
                 conv_ln_g, conv_ln_b, w_out, ple_norm, w_ple_gate, b_ple_gate, w_ple):
    B, S, _ = h.shape
    u = rmsnorm(h, norm_mix)
    proj = u @ w_in
    q, k, v, a_low, g_a, c_val, c_gate, g_b = jnp.split(
        proj, [Q_END, K_END, V_END, ALOW_END, GA_END, CVAL_END, CGATE_END], axis=-1)

    def heads(t, d):
        return t.reshape(B, S, GLA_HEADS, d).transpose(0, 2, 1, 3).astype(jnp.float32)

    log_a = jax.nn.log_sigmoid((a_low @ w_alpha + b_alpha).astype(jnp.float32)) / GATE_TAU
    o = gla_chunked(heads(q, GLA_DK) * (GLA_DK ** -0.5), heads(k, GLA_DK),
                    heads(v, GLA_DV), heads(log_a, GLA_DK))
    o = rmsnorm(o.transpose(0, 2, 1, 3), gla_norm)
    y_a = o.reshape(B, S, GLA_WIDTH).astype(h.dtype) * jax.nn.silu(g_a)

    c = c_val * jax.nn.sigmoid(c_gate)
    c = lax.conv_general_dilated(
        c, conv_w[:, None, :], window_strides=(1,), padding=[(CONV_K - 1, 0)],
        dimension_numbers=('NWC', 'WIO', 'NWC'),
        feature_group_count=CONV_WIDTH) + conv_b
    cg = c.reshape(B, S, CONV_GROUPS, CONV_GROUP_WIDTH).astype(jnp.float32)
    mu = jnp.mean(cg, axis=-1, keepdims=True)
    var = jnp.mean(jnp.square(cg - mu), axis=-1, keepdims=True)
    cg = ((cg - mu) * lax.rsqrt(var + EPS)).reshape(B, S, CONV_WIDTH)
    c = cg * conv_ln_g + conv_ln_b
    y_b = jax.nn.silu(c).astype(h.dtype) * jax.nn.silu(g_b)

    h = h + jnp.concatenate([y_a, y_b], axis=-1) @ w_out

    gate = jax.nn.sigmoid(rmsnorm(h, ple_norm) @ w_ple_gate + b_ple_gate)
    h = h + gate * (p_i @ w_ple)
    return h


def setup_inputs(seed: int = 0) -> dict:
    key = jax.random.key(seed)
    ks = jax.random.split(key, 18)
    f32 = jnp.float32
    nrm = lambda k, shape, s: jax.random.normal(k, shape, f32) * s
    L = DEPTH
    return {
        "x": nrm(ks[0], (BATCH, SEQ, D_MODEL), 1.0),
        "p": nrm(ks[1], (DEPTH, BATCH, SEQ, PLE_DIM), 1.0),
        "norm_mix": 1.0 + nrm(ks[2], (L, D_MODEL), 0.02),
        "w_in": nrm(ks[3], (L, D_MODEL, D_IN), D_MODEL ** -0.5),
        "w_alpha": nrm(ks[4], (L, GATE_RANK, GLA_KEY_WIDTH), GATE_RANK ** -0.5),
        "b_alpha": nrm(ks[5], (L, GLA_KEY_WIDTH), 0.5),
        "gla_norm": 1.0 + nrm(ks[6], (L, GLA_DV), 0.02),
        "conv_w": nrm(ks[7], (L, CONV_K, CONV_WIDTH), CONV_K ** -0.5),
        "conv_b": nrm(ks[8], (L, CONV_WIDTH), 0.02),
        "conv_ln_g": 1.0 + nrm(ks[9], (L, CONV_WIDTH), 0.02),
        "conv_ln_b": nrm(ks[10], (L, CONV_WIDTH), 0.02),
        "w_out": nrm(ks[11], (L, D_MIX, D_MODEL), D_MIX ** -0.5),
        "ple_norm": 1.0 + nrm(ks[12], (L, D_MODEL), 0.02),
        "w_ple_gate": nrm(ks[13], (L, D_MODEL, D_MODEL), D_MODEL ** -0.5),
        "b_ple_gate": nrm(ks[14], (L, D_MODEL), 0.02),
        "w_ple": nrm(ks[15], (L, PLE_DIM, D_MODEL), PLE_DIM ** -0.5),
        "final_norm": 1.0 + nrm(ks[16], (D_MODEL,), 0.02),
    }


def reference(x, p, norm_mix, w_in, w_alpha, b_alpha, gla_norm, conv_w, conv_b,
              conv_ln_g, conv_ln_b, w_out, ple_norm, w_ple_gate, b_ple_gate, w_ple,
              final_norm):
    h = x
    for i in range(DEPTH):
        h = hybrid_layer(h, p[i], norm_mix[i], w_in[i], w_alpha[i], b_alpha[i],
                         gla_norm[i], conv_w[i], conv_b[i], conv_ln_g[i], conv_ln_b[i],
                         w_out[i], ple_norm[i], w_ple_gate[i], b_ple_gate[i], w_ple[i])
    return rmsnorm(h, final_norm)
```

```python
import numpy as np
from contextlib import ExitStack
import concourse.bass as bass
import concourse.mybir as mybir
from concourse.bass_utils import run_bass_kernel_spmd

F32 = mybir.dt.float32
BF16 = mybir.dt.bfloat16
AF = mybir.ActivationFunctionType
ALU = mybir.AluOpType

D = 2048
SEQ = 4096
NB = 4
TB = 1024
NBLK = 4
NPRE = 2
KC = 16
EPS = 1e-6
NSLOT = 6
NBIG = 20992

GID = {}
_g = 0
for h in range(4):
    GID[('q', h)] = _g; _g += 1
for h in range(4):
    GID[('k', h)] = _g; _g += 1
for h in range(4):
    for vc in range(2):
        GID[('v', h, vc)] = _g; _g += 1
for h in range(4):
    for vc in range(2):
        GID[('ga', h, vc)] = _g; _g += 1
for g in range(8):
    GID[('cv', g)] = _g; _g += 1
for g in range(8):
    GID[('cg', g)] = _g; _g += 1
for g in range(8):
    GID[('gb', g)] = _g; _g += 1
for oc in range(16):
    GID[('wo', oc)] = _g; _g += 1
for oc in range(16):
    GID[('wp', oc)] = _g; _g += 1
NG = _g

PP_NM = 0
PP_PN = 16
PP_FN = 32
PP_BPG = 48
PP_BA = 64
PP_GN = 68
PP_CB = 70
PP_LG = 78
PP_LB = 86
PP_CW = 94
NPP = 94 + 8 * 31
PN_BPG = 0
PN_BA = 16
PN_LG = 20
PN_LB = 28
NPN = 36


class Sched:
    def __init__(self, nc, es):
        self.nc = nc
        self.es = es
        self.eng = {'pe': nc.tensor, 'act': nc.scalar, 'dve': nc.vector,
                    'pool': nc.gpsimd, 'sp': nc.sync}
        self.sem = {e: es.enter_context(nc.semaphore('s_' + e))
                    for e in ('pe', 'act', 'dve', 'pool')}
        self.cnt = {e: 0 for e in self.sem}
        self.dsem = {}
        self.dcnt = {}
        self.lastw = {}
        self.readers = {}
        self.seen = {e: {} for e in self.eng}
        self.nwait = 0
        self.trace = {e: [] for e in self.eng}

    def _ln(self):
        import sys
        f = sys._getframe(1)
        out = []
        while f is not None and len(out) < 4:
            if f.f_code.co_name not in ('_ln', '_wait', 'op', 'dma', 'act', '<lambda>'):
                out.append(f.f_lineno)
            f = f.f_back
        return out

    def _deps(self, reads, writes):
        deps = {}

        def add(tok):
            if tok is None:
                return
            nm = tok[0]
            if nm not in deps or deps[nm][1] < tok[1]:
                deps[nm] = tok
        for k in reads:
            add(self.lastw.get(k))
        for k in writes:
            add(self.lastw.get(k))
            for t in self.readers.get(k, {}).values():
                add(t)
        return deps

    def _wait(self, e, deps):
        eng = self.eng[e]
        for nm, tok in deps.items():
            _, val, src, semh = tok
            if src == e and e == 'pe':
                continue
            if self.seen[e].get(nm, 0) >= val:
                continue
            eng.wait_ge(semh, val)
            self.trace[e].append(('w', nm, val, self._ln()))
            self.nwait += 1
            self.seen[e][nm] = val

    def _record(self, tok, reads, writes):
        for k in reads:
            self.readers.setdefault(k, {})[tok[0]] = tok
        for k in writes:
            self.lastw[k] = tok
            self.readers[k] = {}

    def op(self, e, fn, reads=(), writes=(), sig=True):
        reads = list(reads)
        writes = list(writes)
        deps = self._deps(reads, writes)
        self._wait(e, deps)
        ins = fn(self.eng[e])
        if sig:
            self.cnt[e] += 1
            ins.then_inc(self.sem[e], 1)
            tok = (e, self.cnt[e], e, self.sem[e])
            self.trace[e].append(('i', e, 1, self._ln()))
        else:
            tok = (e, self.cnt[e] + 1, e, self.sem[e])
        self._record(tok, reads, writes)

    def dma(self, q, fn, reads, writes, key):
        reads = list(reads)
        writes = list(writes)
        if key not in self.dsem:
            self.dsem[key] = self.es.enter_context(self.nc.semaphore('d_' + key))
            self.dcnt[key] = 0
        deps = self._deps(reads, writes)
        self._wait(q, deps)
        ins = fn(self.eng[q])
        self.dcnt[key] += 16
        ins.then_inc(self.dsem[key], 16)
        self.trace[q].append(('i', 'd_' + key, 16, self._ln()))
        tok = ('d_' + key, self.dcnt[key], 'dma', self.dsem[key])
        self._record(tok, reads, writes)

    def finish(self, e='sp'):
        eng = self.eng[e]
        for key, semh in self.dsem.items():
            if self.dcnt[key] > self.seen[e].get('d_' + key, 0):
                eng.wait_ge(semh, self.dcnt[key])


class DrySched:
    def op(self, *a, **k):
        pass

    def dma(self, *a, **k):
        pass


def run_lanes(lanes, pattern=None):
    lanes = list(lanes)
    alive = [True] * len(lanes)
    if pattern is None:
        pattern = list(range(len(lanes)))
    while any(alive):
        for i in pattern:
            if not alive[i]:
                continue
            try:
                next(lanes[i])
            except StopIteration:
                alive[i] = False


def chain(*gens):
    for g in gens:
        yield from g


def build_nc(debug=None):
    nc = bass.Bass("TRN2", target_bir_lowering=False)
    xT = nc.dram_tensor("xT", [D, 2 * 2048], F32, kind="ExternalInput").ap()
    pT = nc.dram_tensor("pT", [256, 2048], F32, kind="ExternalInput").ap()
    wg = nc.dram_tensor("wg", [NG, 128, KC * 128], F32, kind="ExternalInput").ap()
    walow = nc.dram_tensor("walow", [128, KC * 128], F32, kind="ExternalInput").ap()
    wple = nc.dram_tensor("wple", [128, 2 * D], F32, kind="ExternalInput").ap()
    ppd = nc.dram_tensor("pp", [128, NPP], F32, kind="ExternalInput").ap()
    walpha = nc.dram_tensor("walpha", [16, 512], F32, kind="ExternalInput").ap()
    cst = nc.dram_tensor("cst", [128, 256], F32, kind="ExternalInput").ap()
    outT = nc.dram_tensor("outT", [D, 2048], F32, kind="ExternalOutput").ap()

    es = ExitStack()
    with es:
        def sb(name, shape, dt):
            return es.enter_context(nc.sbuf_tensor(name, shape, dt))

        U = sb("U", [128, KC, TB], BF16)
        Y = sb("Y", [128, KC, TB], BF16)
        BIG = sb("BIG", [128, NBIG], F32)
        W = [sb("W%d" % i, [128, KC, 128], BF16) for i in range(NSLOT)]
        WPLE = sb("WPLE", [128, 2, D], BF16)
        WALOW = sb("WALOW", [128, KC, 128], BF16)
        WA3 = sb("WA3", [128, 512], BF16)
        PP = sb("PP", [128, NPP], F32)
        PN = sb("PN", [128, NPN], F32)
        CST = sb("CST", [128, 256], F32)
        IDB = sb("IDB", [128, 128], BF16)
        ONESB = sb("ONESB", [128, 128], BF16)
        ONESF = sb("ONESF", [128, 128], F32)
        RS = [sb("RSTD0", [128, TB], F32), sb("RSTD1", [128, TB], F32)]
        SQ = sb("SQ", [128, 2, TB], BF16)
        ST = sb("ST", [128, 4, 256], F32)
        HALO = sb("HALO", [128, 8, 32], BF16)
        DSS = sb("DSS", [128, 2, 8], F32)
        PBLK = sb("PBLK", [128, 2, TB], BF16)
        PS = [es.enter_context(nc.psum_tensor("ps%d" % i, [128, 512], F32)) for i in range(7)]
        PT = es.enter_context(nc.psum_tensor("pt", [128, 1024], BF16))

        IDF = CST[:, 0:128]
        MASK = CST[:, 128:256]
        real = Sched(nc, es)
        stream = []

        def bk(a, b):
            return [('big', j) for j in range(a // 256, (b - 1) // 256 + 1)]

        def vf(off, n):
            return BIG[:, off:off + n], bk(off, off + n)

        def vb(off, n_bf):
            n = n_bf // 2
            return BIG[:, off:off + n].bitcast(BF16), bk(off, off + n)

        def ppc(col):
            return PP[:, col:col + 1]

        def pnc(col):
            return PN[:, col:col + 1]

        def emit(S, dry):
            state = {'issued': 0, 'next': 0}

            def issue_w(extra_reads=()):
                i = state['issued']
                slot = i % NSLOT
                gid = GID[stream[i]]
                S.dma('pool', lambda e: e.dma_start(out=W[slot][:].rearrange("p a b -> p (a b)"),
                                                    in_=wg[gid]),
                      list(extra_reads), [('w', slot)], 'w%d' % slot)
                state['issued'] += 1

            def next_w(expect):
                if dry:
                    stream.append(expect)
                    return 0
                i = state['next']
                assert stream[i] == expect, (stream[i], expect)
                while state['issued'] < min(len(stream), i + NSLOT):
                    issue_w()
                state['next'] += 1
                return i % NSLOT

            def act(out, in_, func, reads, writes, bias=None, scale=None):
                kw = {}
                if bias is not None:
                    kw['bias'] = bias
                if scale is not None:
                    kw['scale'] = scale
                S.op('act', lambda e: e.activation(out=out, in_=in_, func=func, **kw), reads, writes)

            def sigmoid_chain(dst, dkeys, src, skeys, scale=-1.0, bias=None):
                act(dst, src, AF.Exp, skeys, dkeys, bias=bias, scale=scale)
                act(dst, dst, AF.Ln, dkeys, dkeys, bias=1.0)
                act(dst, dst, AF.Exp, dkeys, dkeys, scale=-1.0)

            def rstd_from(dst, dkeys, src, skeys, inv_n):
                act(dst, src, AF.Ln, skeys, dkeys, bias=EPS, scale=inv_n)
                act(dst, dst, AF.Exp, dkeys, dkeys, scale=-0.5)

            def proj(slot, banks, rhs_of, rkeys_of, halves=(0, 1), wap=None, M=128, N=512):
                if wap is None:
                    order = [(kc, hf) for hf in halves for kc in range(KC)]
                else:
                    order = [(kc, hf) for kc in range(KC) for hf in halves]
                for kc, hf in order:
                    if True:
                        lhsT = W[slot][:, kc, :] if wap is None else wap(kc)
                        wk = [('w', slot)] if wap is None else ['WALOW']
                        S.op('pe', lambda e: e.matmul(
                            PS[banks[hf]][0:M, 0:N], lhsT, rhs_of(kc, hf),
                            start=(kc == 0), stop=(kc == KC - 1)),
                            wk + rkeys_of(kc), [('ps', banks[hf])], sig=(kc == KC - 1))

            def u_rhs(kc, hf):
                return U[:, kc, hf * 512:(hf + 1) * 512]

            def u_keys(kc):
                return [('U', kc)]

            def y_rhs(kc, hf):
                return Y[:, kc, hf * 512:(hf + 1) * 512]

            def y_keys(kc):
                return [('Y', kc)]

            class Banks:
                def __init__(self, lst):
                    self.lst = lst
                    self.i = 0

                def one(self):
                    b = self.lst[self.i % len(self.lst)]
                    self.i += 1
                    return b

                def pair(self):
                    return (self.one(), self.one())

            if not dry:
                S.dma('sp', lambda e: e.dma_start(out=PP[:], in_=ppd), [], ['PP'], 'c0')
                S.dma('sp', lambda e: e.dma_start(out=CST[:], in_=cst), [], ['CST'], 'c1')
                WA3F = RS[1][:, 0:512]
                k_wa = [('rstd', 1, 0)]
                S.op('dve', lambda e: e.memset(WA3F, 0.0), [], k_wa)
                for r in range(3):
                    S.dma('sp', lambda e: e.dma_start(out=RS[1][32 * r:32 * r + 16, 0:512], in_=walpha),
                          [], k_wa, 'c2')
                S.op('act', lambda e: e.copy(out=WA3[0:80, :], in_=RS[1][0:80, 0:512]), k_wa, ['WA3'])
                S.op('dve', lambda e: e.tensor_tensor(out=WA3[64:80, :], in0=RS[1][64:80, 0:512],
                                                      in1=WA3[64:80, :], op=ALU.subtract), k_wa + ['WA3'], ['WA3'])
                S.dma('pool', lambda e: e.dma_start(out=WALOW[:].rearrange("p a b -> p (a b)"), in_=walow),
                      [], ['WALOW'], 'c3')
                S.op('dve', lambda e: e.memset(ONESB[:], 1.0), [], ['ONESB'])
                S.op('dve', lambda e: e.memset(ONESF[:], 1.0), [], ['ONESF'])
                S.op('dve', lambda e: e.memset(ST[:].rearrange("p a b -> p (a b)"), 0.0), [],
                     ['S0', 'S1', 'S2', 'S3'])
                S.op('dve', lambda e: e.memset(HALO[:].rearrange("p a b -> p (a b)"), 0.0), [],
                     ['halo%d' % g for g in range(8)])
                S.op('dve', lambda e: e.tensor_copy(out=IDB[:], in_=IDF), ['CST'], ['IDB'])
                S.op('act', lambda e: e.mul(out=PN[:, PN_BPG:PN_BPG + 16], in_=PP[:, PP_BPG:PP_BPG + 16],
                                            mul=-1.0), ['PP'], ['PN'])
                S.op('act', lambda e: e.mul(out=PN[:, PN_BA:PN_BA + 4], in_=PP[:, PP_BA:PP_BA + 4],
                                            mul=-1.0), ['PP'], ['PN'])

            cur = {}

            def gla(h, o, bl, main, lane):
                B_ = Banks(bl)
                if main:
                    SP, k_sp = vf(o + 0, 1024)
                    Bt, k_b = vf(o + 1024, 1024)
                    E1, k_e1 = vf(o + 2048, 1024)
                    E2, k_e2 = vf(o + 3072, 1024)
                    QD, k_qd = vb(o + 4096, 1024)
                    KD, k_kd = vb(o + 4608, 1024)
                    KD2T, k_kd2t = vb(o + 5120, 1024)
                    VT, k_vt = vb(o + 5632, 2048)
                    VTOK, k_vtok = vb(o + 6656, 2048)
                    KTOK, k_ktok = vb(o + 7680, 1024)
                    SGA, k_sga = vb(o + 8192, 2048)
                    SF, k_sf = vf(o + 9216, 1792)
                    SBf, k_sb = vb(o + 11008, 2048)
                    Rr, k_r = vf(o + 12032, 512)
                    SGT, k_sgt = SP, k_sp
                    Tt, k_t = E1, k_e1
                    STB, k_stb = vb(o + 5632, 1024)
                    SQO, k_sqo = vb(o + 6144, 1024)
                    SGA = SGA.rearrange("p (a b) -> p a b", a=2)
                    SBf = SBf.rearrange("p (a b) -> p a b", a=8)
                    SQO = SQO.rearrange("p (a b) -> p a b", a=2)
                    Tt = Tt.rearrange("p (a b) -> p a b", a=2)
                else:
                    SP, k_sp = vf(o + 0, 1024)
                    Bt, k_b = vf(o + 1024, 1024)
                    E2, k_e2 = vf(o + 2048, 1024)
                    KD2T, k_kd2t = vb(o + 3072, 1024)
                    VT, k_vt = vb(o + 3584, 2048)
                    VTOK, k_vtok = vb(o + 4608, 2048)
                    KTOK, k_ktok = vb(o + 5632, 1024)
                    SF, k_sf = vf(o + 6144, 1792)
                VT = VT.rearrange("p (a b) -> p a b", a=2)
                VTOK = VTOK.rearrange("p (a b) -> p a b", a=8)
                KTOK = KTOK.rearrange("p (a b) -> p a b", a=8)
                SF = SF.rearrange("p (a b) -> p a b", a=7)
                Sh = ST[:, h, :]
                k_s = ['S%d' % h]
                DS = DSS[:, lane, :]
                k_ds = ['DS%d' % lane]
                k_pt = ['pt']

                bz = B_.pair()
                for hf in range(2):
                    S.op('pe', lambda e: e.matmul(
                        PS[bz[hf]][:, :], WA3[0:80, h * 128:(h + 1) * 128],
                        cur['ALOW'][0:80, hf * 512:(hf + 1) * 512], start=True, stop=True),
                        ['WA3'] + cur['k_alow'], [('ps', bz[hf])])
                for hf in range(2):
                    sl = slice(hf * 512, (hf + 1) * 512)
                    act(SP[:, sl], PS[bz[hf]][:, :], AF.Exp, [('ps', bz[hf]), 'PN'], k_sp,
                        bias=pnc(PN_BA + h), scale=-1.0)
                act(SP[:, :], SP[:, :], AF.Ln, k_sp, k_sp, bias=1.0)
                for c in range(8):
                    S.op('dve', lambda e: e.tensor_tensor_scan(
                        out=Bt[:, c * 128:(c + 1) * 128], data0=ONESF[:, :],
                        data1=SP[:, c * 128:(c + 1) * 128], initial=0.0,
                        op0=ALU.mult, op1=ALU.add), k_sp + ['ONESF'], k_b)
                if main:
                    act(E1[:, :], Bt[:, :], AF.Exp, k_b, k_e1, scale=-1.0 / 16.0)
                act(E2[:, :], Bt[:, :], AF.Exp, k_b, k_e2, scale=1.0 / 16.0)
                act(DS, Bt.rearrange("p (c t) -> p c t", t=128)[:, :, 127], AF.Exp,
                    k_b, k_ds, scale=-1.0 / 16.0)
                yield
                if main:
                    slot = next_w(('q', h))
                    bq = B_.pair()
                    proj(slot, bq, u_rhs, u_keys)
                    for hf in range(2):
                        sl = slice(hf * 512, (hf + 1) * 512)
                        S.op('dve', lambda e: e.scalar_tensor_tensor(
                            out=QD[:, sl], in0=PS[bq[hf]][:, :], scalar=float(128 ** -0.5),
                            in1=E1[:, sl], op0=ALU.mult, op1=ALU.mult),
                            [('ps', bq[hf])] + k_e1, k_qd)
                    yield
                slot = next_w(('k', h))
                bkk = B_.pair()
                proj(slot, bkk, u_rhs, u_keys)
                for hf in range(2):
                    sl = slice(hf * 512, (hf + 1) * 512)
                    if main:
                        S.op('dve', lambda e: e.tensor_tensor(
                            out=KD[:, sl], in0=PS[bkk[hf]][:, :], in1=E2[:, sl], op=ALU.mult),
                            [('ps', bkk[hf])] + k_e2, k_kd)
                    for cc in range(4):
                        c = hf * 4 + cc
                        S.op('dve', lambda e: e.scalar_tensor_tensor(
                            out=KD2T[:, c * 128:(c + 1) * 128],
                            in0=PS[bkk[hf]][:, cc * 128:(cc + 1) * 128],
                            scalar=DS[:, c:c + 1], in1=E2[:, c * 128:(c + 1) * 128],
                            op0=ALU.mult, op1=ALU.mult),
                            [('ps', bkk[hf])] + k_ds + k_e2, k_kd2t)
                yield
                for vc in range(2):
                    slot = next_w(('v', h, vc))
                    bv = B_.pair()
                    proj(slot, bv, u_rhs, u_keys)
                    for hf in range(2):
                        act(VT[:, vc, hf * 512:(hf + 1) * 512], PS[bv[hf]][:, :], AF.Copy,
                            [('ps', bv[hf])], k_vt)
                    yield
                for c in range(8):
                    S.op('pe', lambda e: e.transpose(
                        out=PT[:, c * 128:(c + 1) * 128], in_=KD2T[:, c * 128:(c + 1) * 128],
                        identity=IDB[:, :]), k_kd2t + ['IDB'], k_pt, sig=(c == 7))
                S.op('act', lambda e: e.copy(out=KTOK.rearrange("p a b -> p (a b)"), in_=PT[:, :]),
                     k_pt, k_ktok)
                if main:
                    slot = next_w(('ga', h, 0))
                    bg = B_.pair()
                    proj(slot, bg, u_rhs, u_keys)
                    for hf in range(2):
                        sl = slice(hf * 512, (hf + 1) * 512)
                        sigmoid_chain(SGT[:, sl], k_sgt, PS[bg[hf]][:, :], [('ps', bg[hf])])
                        S.op('dve', lambda e: e.tensor_tensor(
                            out=SGA[:, 0, sl], in0=PS[bg[hf]][:, :], in1=SGT[:, sl],
                            op=ALU.mult), [('ps', bg[hf])] + k_sgt, k_sga)
                yield
                for vc in range(2):
                    for c in range(8):
                        S.op('pe', lambda e: e.transpose(
                            out=PT[:, c * 128:(c + 1) * 128], in_=VT[:, vc, c * 128:(c + 1) * 128],
                            identity=IDB[:, :]), k_vt + ['IDB'], k_pt, sig=(c == 7))
                    S.op('dve', lambda e: e.tensor_copy(
                        out=VTOK[:, :, vc * 128:(vc + 1) * 128],
                        in_=PT[:, :].rearrange("p (a b) -> p a b", a=8)), k_pt, k_vtok)
                    if vc == 0:
                        if main:
                            slot = next_w(('ga', h, 1))
                            bg = B_.pair()
                            proj(slot, bg, u_rhs, u_keys)
                            for hf in range(2):
                                sl = slice(hf * 512, (hf + 1) * 512)
                                sigmoid_chain(SGT[:, sl], k_sgt, PS[bg[hf]][:, :], [('ps', bg[hf])])
                                S.op('dve', lambda e: e.tensor_tensor(
                                    out=SGA[:, 1, sl], in0=PS[bg[hf]][:, :], in1=SGT[:, sl],
                                    op=ALU.mult), [('ps', bg[hf])] + k_sgt, k_sga)
                        yield
                yield
                ub = [B_.one() for _ in range(len(bl))]
                nub = len(ub)
                if main:
                    act(SBf[:, 0, :], Sh, AF.Copy, k_s, k_sb)
                for c in range(8):
                    bnk = ub[(c // 2) % nub]
                    S.op('pe', lambda e: e.matmul(
                        PS[bnk][:, (c % 2) * 256:(c % 2 + 1) * 256], KTOK[:, c, :], VTOK[:, c, :],
                        start=True, stop=True), k_ktok + k_vtok, [('ps', bnk)])
                    src = Sh if c == 0 else SF[:, c - 1, :]
                    dst = Sh if c == 7 else SF[:, c, :]
                    rk = (k_s if c == 0 else k_sf)
                    wk = (k_s if c == 7 else k_sf)
                    S.op('dve', lambda e: e.scalar_tensor_tensor(
                        out=dst, in0=src, scalar=DS[:, c:c + 1],
                        in1=PS[bnk][:, (c % 2) * 256:(c % 2 + 1) * 256],
                        op0=ALU.mult, op1=ALU.add),
                        rk + k_ds + [('ps', bnk)], wk)
                    if c % 2 == 1:
                        yield
                if not main:
                    return
                act(SBf[:, 1:8, :], SF[:, 0:7, :], AF.Copy, k_sf, k_sb)
                sbk = [B_.one(), B_.one()]
                for c in range(8):
                    S.op('pe', lambda e: e.matmul(
                        PS[sbk[c // 4]][:, (c % 4) * 128:(c % 4 + 1) * 128],
                        KD[:, c * 128:(c + 1) * 128], QD[:, c * 128:(c + 1) * 128],
                        start=True, stop=True), k_kd + k_qd, [('ps', sbk[c // 4])])
                for hf in range(2):
                    S.op('dve', lambda e: e.tensor_tensor(
                        out=STB[:, hf * 512:(hf + 1) * 512].rearrange("p (a b) -> p a b", a=4),
                        in0=PS[sbk[hf]][:, :].rearrange("p (a b) -> p a b", a=4),
                        in1=MASK.unsqueeze(1).to_broadcast([128, 4, 128]), op=ALU.mult),
                        [('ps', sbk[hf]), 'CST'], k_stb)
                yield
                for hf in range(2):
                    obk = [B_.one(), B_.one()]
                    for vc in range(2):
                        bo = obk[vc]
                        for cc in range(4):
                            c = hf * 4 + cc
                            S.op('pe', lambda e: e.matmul(
                                PS[bo][:, cc * 128:(cc + 1) * 128],
                                VTOK[:, c, vc * 128:(vc + 1) * 128], STB[:, c * 128:(c + 1) * 128],
                                start=True, stop=False), k_vtok + k_stb, [('ps', bo)], sig=False)
                            S.op('pe', lambda e: e.matmul(
                                PS[bo][:, cc * 128:(cc + 1) * 128],
                                SBf[:, c, vc * 128:(vc + 1) * 128], QD[:, c * 128:(c + 1) * 128],
                                start=False, stop=True), k_sb + k_qd, [('ps', bo)], sig=(cc == 3))
                        act(SQO[:, vc, :], PS[bo][:, :], AF.Square, [('ps', bo)], k_sqo)
                    yield
                    sbank = B_.one()
                    for vc in range(2):
                        S.op('pe', lambda e: e.matmul(
                            PS[sbank][:, :], ONESB[:, :], SQO[:, vc, :], start=(vc == 0), stop=(vc == 1)),
                            k_sqo + ['ONESB'], [('ps', sbank)], sig=True)
                    rstd_from(Rr[:, :], k_r, PS[sbank][:, :], [('ps', sbank)], 1.0 / 256.0)
                    for vc in range(2):
                        bo = obk[vc]
                        sl = slice(hf * 512, (hf + 1) * 512)
                        S.op('dve', lambda e: e.scalar_tensor_tensor(
                            out=Tt[:, vc, :], in0=PS[bo][:, :], scalar=ppc(PP_GN + vc), in1=Rr[:, :],
                            op0=ALU.mult, op1=ALU.mult), [('ps', bo), 'PP'] + k_r, k_t)
                        S.op('dve', lambda e: e.tensor_tensor(
                            out=Y[:, 2 * h + vc, sl], in0=Tt[:, vc, :], in1=SGA[:, vc, sl],
                            op=ALU.mult), k_t + k_sga, [('Y', 2 * h + vc)])
                    yield

            def conv(g, o, bl, main, first=False):
                B_ = Banks(bl)
                hk = ['halo%d' % g]
                if not main:
                    SIG, k_sig = vf(o + 0, 256)
                    Cb, k_c = vb(o + 256, 512)
                    slot = next_w(('cg', g))
                    b1 = B_.one()
                    proj(slot, (b1, b1), lambda kc, hf: U[:, kc, 896:1024], u_keys, halves=(0,), N=128)
                    sigmoid_chain(SIG[:, 0:128], k_sig, PS[b1][:, 0:128], [('ps', b1)])
                    yield
                    slot = next_w(('cv', g))
                    b2 = B_.one()
                    proj(slot, (b2, b2), lambda kc, hf: U[:, kc, 896:1024], u_keys, halves=(0,), N=128)
                    S.op('dve', lambda e: e.tensor_tensor(
                        out=Cb[:, 0:128], in0=PS[b2][:, 0:128], in1=SIG[:, 0:128], op=ALU.mult),
                        [('ps', b2)] + k_sig, k_c)
                    S.op('dve', lambda e: e.tensor_copy(out=HALO[:, g, 0:30], in_=Cb[:, 98:128]),
                         k_c, hk)
                    yield
                    return
                SIG, k_sig = vf(o + 0, 1024)
                Cb, k_c = vb(o + 1024, 1536)
                SGB, k_sgb = vb(o + 1792, 1024)
                SG2, k_sg2 = SIG, k_sig
                TAPS, k_taps = vb(o + 2304, 4096)
                TAPS = TAPS.rearrange("p (a b) -> p a b", a=32)
                XC, k_xc = vf(o + 4352, 512)
                XCB, k_xcb = vb(o + 4864, 512)
                Dd, k_d = vf(o + 5120, 512)
                SQC, k_sqc = vb(o + 5632, 512)
                RC, k_rc = vf(o + 5888, 512)
                A0, k_a0 = vf(o + 6400, 512)
                SG3, k_sg3 = vf(o + 6912, 512)
                UH = PBLK[:, :, :].rearrange("p a b -> p (a b)").rearrange("p (a b) -> p a b", a=KC)
                k_uh = [('pb', 0), ('pb', 1)]
                SIGX, k_sigx = vf(o + 4352, 128)
                slot = next_w(('cg', g))
                b1 = B_.pair()
                proj(slot, b1, u_rhs, u_keys)
                if first:
                    bx = B_.one()
                    for kc in range(KC):
                        S.op('pe', lambda e: e.matmul(PS[bx][:, 0:128], W[slot][:, kc, :], UH[:, kc, :],
                                                      start=(kc == 0), stop=(kc == KC - 1)),
                             [('w', slot)] + k_uh, [('ps', bx)], sig=(kc == KC - 1))
                    sigmoid_chain(SIGX[:, :], k_sigx, PS[bx][:, 0:128], [('ps', bx)])
                for hf in range(2):
                    sl = slice(hf * 512, (hf + 1) * 512)
                    sigmoid_chain(SIG[:, sl], k_sig, PS[b1[hf]][:, :], [('ps', b1[hf])])
                S.op('dve', lambda e: e.tensor_tensor(
                    out=TAPS[:, 0:31, :],
                    in0=IDF.unsqueeze(1).to_broadcast([128, 31, 128]),
                    in1=PP[:, PP_CW + g * 31:PP_CW + (g + 1) * 31].unsqueeze(2).to_broadcast([128, 31, 128]),
                    op=ALU.mult), ['CST', 'PP'], k_taps)
                yield
                slot = next_w(('cv', g))
                b2 = B_.pair()
                proj(slot, b2, u_rhs, u_keys)
                if first:
                    bx2 = B_.one()
                    for kc in range(KC):
                        S.op('pe', lambda e: e.matmul(PS[bx2][:, 0:128], W[slot][:, kc, :], UH[:, kc, :],
                                                      start=(kc == 0), stop=(kc == KC - 1)),
                             [('w', slot)] + k_uh, [('ps', bx2)], sig=(kc == KC - 1))
                    S.op('dve', lambda e: e.tensor_tensor(
                        out=Cb[:, 0:30], in0=PS[bx2][:, 98:128], in1=SIGX[:, 98:128], op=ALU.mult),
                        [('ps', bx2)] + k_sigx, k_c)
                else:
                    S.op('dve', lambda e: e.tensor_copy(out=Cb[:, 0:30], in_=HALO[:, g, 0:30]), hk, k_c)
                for hf in range(2):
                    sl = slice(hf * 512, (hf + 1) * 512)
                    S.op('dve', lambda e: e.tensor_tensor(
                        out=Cb[:, 30 + hf * 512:30 + (hf + 1) * 512], in0=PS[b2[hf]][:, :],
                        in1=SIG[:, sl], op=ALU.mult), [('ps', b2[hf])] + k_sig, k_c)
                S.op('dve', lambda e: e.tensor_copy(out=HALO[:, g, 0:30], in_=Cb[:, 1024:1054]), k_c, hk)
                yield
                slot = next_w(('gb', g))
                b3 = B_.pair()
                proj(slot, b3, u_rhs, u_keys)
                for hf in range(2):
                    sl = slice(hf * 512, (hf + 1) * 512)
                    sigmoid_chain(SG2[:, sl], k_sg2, PS[b3[hf]][:, :], [('ps', b3[hf])])
                    S.op('dve', lambda e: e.tensor_tensor(
                        out=SGB[:, sl], in0=PS[b3[hf]][:, :], in1=SG2[:, sl], op=ALU.mult),
                        [('ps', b3[hf])] + k_sg2, k_sgb)
                yield
                XCs = [(XC, k_xc, XCB, k_xcb)]
                XC2, k_xc2 = vf(o + 7424, 512)
                XCB2, k_xcb2 = vb(o + 7936, 512)
                XCs.append((XC2, k_xc2, XCB2, k_xcb2))
                for hf in range(2):
                    XCh, k_xch, XCBh, k_xcbh = XCs[hf]
                    ba = B_.one()
                    for k in range(31):
                        S.op('pe', lambda e: e.matmul(
                            PS[ba][:, :], TAPS[:, k, :], Cb[:, hf * 512 + k:hf * 512 + k + 512],
                            start=(k == 0), stop=(k == 30)), k_taps + k_c, [('ps', ba)],
                            sig=(k == 30))
                    act(XCh[:, :], PS[ba][:, :], AF.Identity, [('ps', ba), 'PP'], k_xch,
                        bias=ppc(PP_CB + g))
                    act(XCBh[:, :], PS[ba][:, :], AF.Identity, [('ps', ba), 'PP'], k_xcbh,
                        bias=ppc(PP_CB + g))
                    yield
                for hf in range(2):
                    XCh, k_xch, XCBh, k_xcbh = XCs[hf]
                    sl = slice(hf * 512, (hf + 1) * 512)
                    bm = B_.one()
                    S.op('pe', lambda e: e.matmul(PS[bm][:, :], ONESB[:, :], XCBh[:, :],
                                                  start=True, stop=True),
                         k_xcbh + ['ONESB'], [('ps', bm)])
                    S.op('dve', lambda e: e.scalar_tensor_tensor(
                        out=Dd[:, :], in0=PS[bm][:, :], scalar=-1.0 / 128.0, in1=XCh[:, :],
                        op0=ALU.mult, op1=ALU.add), [('ps', bm)] + k_xch, k_d)
                    act(SQC[:, :], Dd[:, :], AF.Square, k_d, k_sqc)
                    yield
                    bv = B_.one()
                    S.op('pe', lambda e: e.matmul(PS[bv][:, :], ONESB[:, :], SQC[:, :],
                                                  start=True, stop=True),
                         k_sqc + ['ONESB'], [('ps', bv)])
                    rstd_from(RC[:, :], k_rc, PS[bv][:, :], [('ps', bv)], 1.0 / 128.0)
                    S.op('dve', lambda e: e.tensor_tensor(out=Dd[:, :], in0=Dd[:, :], in1=RC[:, :],
                                                          op=ALU.mult), k_d + k_rc, k_d)
                    act(A0[:, :], Dd[:, :], AF.Identity, k_d + ['PP'], k_a0,
                        bias=ppc(PP_LB + g), scale=ppc(PP_LG + g))
                    sigmoid_chain(SG3[:, :], k_sg3, A0[:, :], k_a0)
                    S.op('dve', lambda e: e.tensor_tensor(out=A0[:, :], in0=A0[:, :], in1=SG3[:, :],
                                                          op=ALU.mult), k_a0 + k_sg3, k_a0)
                    S.op('dve', lambda e: e.tensor_tensor(
                        out=Y[:, 8 + g, sl], in0=A0[:, :], in1=SGB[:, sl], op=ALU.mult),
                        k_a0 + k_sgb, [('Y', 8 + g)])
                    yield

            XS = [vf(18432, 1024), vf(19456, 1024)]
            XSH = [vf(18432, 512), vf(18944, 512), vf(19456, 512)]
            SQN = [vb(19968, 512), vb(20224, 512)]
            PTF = PT[:, :].bitcast(F32)

            def xrows(kc):
                return slice(kc * 128, (kc + 1) * 128)

            def a1_stats(blk, bank, bkey):
                t0 = blk * TB
                R = RS[blk % 2]
                steps = [(hf, kc) for hf in range(2) for kc in range(KC)]
                n = len(steps)

                def dma_i(i):
                    hf, kc = steps[i]
                    xs, k_xs = XSH[i % 3]
                    S.dma('sp', lambda e: e.dma_start(
                        out=xs[:, :], in_=xT[xrows(kc), t0 + hf * 512:t0 + (hf + 1) * 512]),
                        [], k_xs, 'xs%d' % (i % 3))

                def sq_i(i):
                    xs, k_xs = XSH[i % 3]
                    sq, k_sq = SQN[i % 2]
                    act(sq[:, :], xs[:, :], AF.Square, k_xs, k_sq)

                def mm_i(i):
                    hf, kc = steps[i]
                    sq, k_sq = SQN[i % 2]
                    S.op('pe', lambda e: e.matmul(bank, ONESB[:, :], sq[:, :],
                                                  start=(kc == 0), stop=(kc == KC - 1)),
                         k_sq + ['ONESB'], [bkey], sig=True)
                    if kc == KC - 1:
                        rstd_from(R[:, hf * 512:(hf + 1) * 512], [('rstd', blk % 2, hf)], bank, [bkey],
                                  1.0 / D)

                dma_i(0)
                dma_i(1)
                for i in range(n + 1):
                    if i + 2 < n:
                        dma_i(i + 2)
                    if i < n:
                        sq_i(i)
                    if i >= 1:
                        mm_i(i - 1)
                    if i % 2 == 1:
                        yield

            def a1_norm(blk):
                t0 = blk * TB
                R = RS[blk % 2]

                def dma_k(kc):
                    xs, k_xs = XS[kc % 2]
                    S.dma('sp', lambda e: e.dma_start(out=xs[:, :], in_=xT[xrows(kc), t0:t0 + TB]),
                          [], k_xs, 'xn%d' % (kc % 2))

                dma_k(0)
                for kc in range(KC):
                    if kc + 1 < KC:
                        dma_k(kc + 1)
                    xs, k_xs = XS[kc % 2]
                    S.op('dve', lambda e: e.scalar_tensor_tensor(
                        out=U[:, kc, :], in0=xs[:, :], scalar=ppc(PP_NM + kc), in1=R[:, :],
                        op0=ALU.mult, op1=ALU.mult),
                        k_xs + [('rstd', blk % 2, 0), ('rstd', blk % 2, 1), 'PP'], [('U', kc)])
                    if kc % 2 == 1:
                        yield

            def a1_full(blk):
                t0 = blk * TB
                R = RS[blk % 2]
                rkk = [('rstd', blk % 2, 0), ('rstd', blk % 2, 1)]
                Xv = BIG[:, 0:KC * TB].rearrange("p (a b) -> p a b", a=KC)

                def xk(kc):
                    return bk(kc * TB, (kc + 1) * TB)
                for kc in range(KC):
                    S.dma('sp', lambda e: e.dma_start(out=Xv[:, kc, :], in_=xT[xrows(kc), t0:t0 + TB]),
                          [], xk(kc), 'x%d' % kc)
                for kc in range(KC):
                    sq = SQ[:, kc % 2, :]
                    if kc % 2 == 0:
                        act(sq, Xv[:, kc, :], AF.Square, xk(kc), [('sq', kc % 2)])
                    else:
                        S.op('dve', lambda e: e.tensor_tensor(out=sq, in0=Xv[:, kc, :], in1=Xv[:, kc, :],
                                                              op=ALU.mult), xk(kc), [('sq', kc % 2)])
                    for hf in range(2):
                        S.op('pe', lambda e: e.matmul(
                            PS[5 + hf][:, :], ONESB[:, :], sq[:, hf * 512:(hf + 1) * 512],
                            start=(kc == 0), stop=(kc == KC - 1)),
                            [('sq', kc % 2), 'ONESB'], [('ps', 5 + hf)], sig=True)
                for hf in range(2):
                    rstd_from(R[:, hf * 512:(hf + 1) * 512], [rkk[hf]], PS[5 + hf][:, :],
                              [('ps', 5 + hf)], 1.0 / D)
                for kc in range(KC):
                    S.op('dve', lambda e: e.scalar_tensor_tensor(
                        out=U[:, kc, :], in0=Xv[:, kc, :], scalar=ppc(PP_NM + kc), in1=R[:, :],
                        op0=ALU.mult, op1=ALU.mult),
                        xk(kc) + rkk + ['PP'], [('U', kc)])

            a1_full(0)
            if not dry:
                for _ in range(NSLOT):
                    issue_w(extra_reads=bk(15 * TB, 16 * TB))
            S.dma('pool', lambda e: e.dma_start(out=WPLE[:].rearrange("p a b -> p (a b)"), in_=wple),
                  [], ['WPLE'], 'c4')

            for blk in range(NBLK):
                main = blk >= NPRE
                t0 = blk * TB
                RSTD = RS[blk % 2]
                rk = [('rstd', blk % 2, 0), ('rstd', blk % 2, 1)]
                ALOW = RSTD[:, 0:512].bitcast(BF16)
                k_alow = ['alow'] + rk
                cur['ALOW'] = ALOW
                cur['k_alow'] = k_alow
                Hv = BIG[:, 0:KC * TB].rearrange("p (a b) -> p a b", a=KC)

                def xkeys(kc):
                    return bk(kc * TB, (kc + 1) * TB)

                proj(None, (0, 1), u_rhs, u_keys, wap=lambda kc: WALOW[:, kc, :])
                for hf in range(2):
                    sl = slice(hf * 512, (hf + 1) * 512)
                    act(ALOW[0:80, sl], PS[hf][0:80, :], AF.Copy, [('ps', hf)], k_alow)
                    S.op('dve', lambda e: e.tensor_tensor(out=ALOW[32:48, sl], in0=PS[hf][32:48, :],
                                                          in1=ALOW[32:48, sl], op=ALU.subtract),
                         [('ps', hf)] + k_alow, k_alow)

                if main:
                    lane1 = chain(*[gla(h, 0, [0, 1, 2, 3], True, 0) for h in range(4)])
                    fst = (blk == NPRE)
                    lane2 = chain(*[conv(g, 12544, [4, 5, 6], True, fst) for g in range(6)])
                    run_lanes([lane1, lane2])
                    run_lanes([conv(6, 0, [0, 1, 2, 3], True, fst), conv(7, 12544, [4, 5, 6], True, fst)])
                else:
                    tasksA = [gla(0, 0, [0, 1, 2, 3], False, 0), gla(2, 0, [0, 1, 2, 3], False, 0)]
                    tasksB = [gla(1, 7936, [4, 5, 6], False, 1), gla(3, 7936, [4, 5, 6], False, 1)]
                    run_lanes([chain(*tasksA), chain(*tasksB)])
                    if blk == NPRE - 1:
                        UHs = PBLK[:, :, :].rearrange("p a b -> p (a b)").rearrange("p (a b) -> p a b", a=KC)
                        S.op('act', lambda e: e.copy(out=UHs, in_=U[:, :, 896:1024]),
                             [('U', kc) for kc in range(KC)], [('pb', 0), ('pb', 1)])
                    a1_full(blk + 1)
                    continue

                tm = (blk - NPRE) * TB
                for kc2 in range(2):
                    S.dma('pool', lambda e: e.dma_start(
                        out=PBLK[:, kc2, :], in_=pT[kc2 * 128:(kc2 + 1) * 128, tm:tm + TB]),
                        [], [('pb', kc2)], 'p%d' % kc2)
                for oc in range(KC):
                    S.dma('sp', lambda e: e.dma_start(out=Hv[:, oc, :],
                                                      in_=xT[oc * 128:(oc + 1) * 128, t0:t0 + TB]),
                          [], xkeys(oc), 'x%d' % oc)

                cnts = {'n': 0}

                def stats_sq(oc):
                    sq = SQ[:, oc % 2, :]
                    act(sq, Hv[:, oc, :], AF.Square, xkeys(oc), [('sq', oc % 2)])

                def stats_mm(oc, do_sq=True):
                    cnts['n'] += 1
                    first = cnts['n'] == 1
                    last = cnts['n'] == KC
                    sq = SQ[:, oc % 2, :]
                    if do_sq:
                        stats_sq(oc)
                    for hf in range(2):
                        S.op('pe', lambda e: e.matmul(
                            PS[5 + hf][:, :], ONESB[:, :], sq[:, hf * 512:(hf + 1) * 512],
                            start=first, stop=last),
                            [('sq', oc % 2), 'ONESB'], [('ps', 5 + hf)], sig=True)

                def b1(ocs, banks):
                    pend = None
                    for oc in ocs:
                        slot = next_w(('wo', oc))
                        proj(slot, banks, y_rhs, y_keys)
                        if pend is not None:
                            stats_mm(pend)
                        yield
                        for hf in range(2):
                            sl = slice(hf * 512, (hf + 1) * 512)
                            S.op('dve', lambda e: e.tensor_tensor(
                                out=Hv[:, oc, sl], in0=Hv[:, oc, sl], in1=PS[banks[hf]][:, :], op=ALU.add),
                                xkeys(oc) + [('ps', banks[hf])], xkeys(oc))
                        pend = oc
                    yield
                    stats_mm(pend)

                cnts['n'] = 0
                lanesB = [b1(range(0, KC, 2), (0, 1)), b1(range(1, KC, 2), (2, 3))]
                nxt = blk + 1 < NBLK
                if nxt:
                    lanesB.append(a1_stats(blk + 1, PTF, 'pt'))
                run_lanes(lanesB)
                for hf in range(2):
                    rstd_from(RSTD[:, hf * 512:(hf + 1) * 512], [rk[hf]], PS[5 + hf][:, :],
                              [('ps', 5 + hf)], 1.0 / D)
                for kc in range(KC):
                    S.op('dve', lambda e: e.scalar_tensor_tensor(
                        out=Y[:, kc, :], in0=Hv[:, kc, :], scalar=ppc(PP_PN + kc), in1=RSTD[:, :],
                        op0=ALU.mult, op1=ALU.mult),
                        xkeys(kc) + rk + ['PP'], [('Y', kc)])

                def b2(ocs, banks, o, pbank):
                    G0, k_g0 = vf(o, 1024)
                    pend = None
                    for oc in ocs:
                        if pend is not None:
                            stats_sq(pend)
                        slot = next_w(('wp', oc))
                        proj(slot, banks, y_rhs, y_keys)
                        if pend is not None:
                            stats_mm(pend, do_sq=False)
                        pbs = []
                        for hf in range(2):
                            pb_ap = PS[4][:, :] if hf == 0 else PTF
                            pb_k = ('ps', 4) if hf == 0 else 'pt'
                            pbs.append((pb_ap, pb_k))
                            for kc2 in range(2):
                                S.op('pe', lambda e: e.matmul(
                                    pb_ap, WPLE[:, kc2, oc * 128:(oc + 1) * 128],
                                    PBLK[:, kc2, hf * 512:(hf + 1) * 512], start=(kc2 == 0), stop=(kc2 == 1)),
                                    ['WPLE', ('pb', kc2)], [pb_k], sig=(kc2 == 1))
                        for hf in range(2):
                            sl = slice(hf * 512, (hf + 1) * 512)
                            sigmoid_chain(G0[:, sl], k_g0, PS[banks[hf]][:, :], [('ps', banks[hf]), 'PN'],
                                          scale=-1.0, bias=pnc(PN_BPG + oc))
                        for hf in range(2):
                            sl = slice(hf * 512, (hf + 1) * 512)
                            pb_ap, pb_k = pbs[hf]
                            S.op('dve', lambda e: e.tensor_tensor(
                                out=G0[:, sl], in0=G0[:, sl], in1=pb_ap, op=ALU.mult),
                                k_g0 + [pb_k], k_g0)
                            S.op('dve', lambda e: e.tensor_tensor(
                                out=Hv[:, oc, sl], in0=Hv[:, oc, sl], in1=G0[:, sl], op=ALU.add),
                                xkeys(oc) + k_g0, xkeys(oc))
                        pend = oc
                        yield
                    stats_mm(pend)

                cnts['n'] = 0
                lanesB = [b2(range(0, KC, 2), (0, 1), 16384, 4), b2(range(1, KC, 2), (2, 3), 17408, 4)]
                if nxt:
                    lanesB.append(a1_norm(blk + 1))
                run_lanes(lanesB)
                for hf in range(2):
                    rstd_from(RSTD[:, hf * 512:(hf + 1) * 512], [rk[hf]], PS[5 + hf][:, :],
                              [('ps', 5 + hf)], 1.0 / D)
                for oc in range(KC):
                    S.op('dve', lambda e: e.scalar_tensor_tensor(
                        out=Hv[:, oc, :], in0=Hv[:, oc, :], scalar=ppc(PP_FN + oc), in1=RSTD[:, :],
                        op0=ALU.mult, op1=ALU.mult),
                        xkeys(oc) + rk + ['PP'], xkeys(oc))
                    S.dma('sp', lambda e: e.dma_start(
                        out=outT[oc * 128:(oc + 1) * 128, tm:tm + TB], in_=Hv[:, oc, :]),
                        xkeys(oc), [], 'o%d' % oc)
            if not dry:
                assert state['next'] == len(stream), (state['next'], len(stream))

        emit(DrySched(), True)
        emit(real, False)
        real.finish('sp')
        build_nc.last_sched = real
    return nc


def _pack_groups(w_in, w_out, w_pg):
    wg = np.empty((NG, 128, KC * 128), np.float32)

    def put(gid, cols):
        wg[gid] = cols.reshape(KC, 128, 128).transpose(1, 0, 2).reshape(128, KC * 128)
    for h in range(4):
        put(GID[('q', h)], w_in[:, h * 128:(h + 1) * 128])
        put(GID[('k', h)], w_in[:, 512 + h * 128:512 + (h + 1) * 128])
        for vc in range(2):
            c0 = 1024 + h * 256 + vc * 128
            put(GID[('v', h, vc)], w_in[:, c0:c0 + 128])
            c0 = 2064 + h * 256 + vc * 128
            put(GID[('ga', h, vc)], w_in[:, c0:c0 + 128])
    for g in range(8):
        put(GID[('cv', g)], w_in[:, 3088 + g * 128:3088 + (g + 1) * 128])
        put(GID[('cg', g)], w_in[:, 4112 + g * 128:4112 + (g + 1) * 128])
        put(GID[('gb', g)], w_in[:, 5136 + g * 128:5136 + (g + 1) * 128])
    for oc in range(16):
        put(GID[('wo', oc)], w_out[:, oc * 128:(oc + 1) * 128])
        put(GID[('wp', oc)], w_pg[:, oc * 128:(oc + 1) * 128])
    return wg


def _vec16(v):
    return np.ascontiguousarray(v.reshape(-1, 128).T)


def _prep_inputs(x, p, norm_mix, w_in, w_alpha, b_alpha, gla_norm, conv_w, conv_b,
                 conv_ln_g, conv_ln_b, w_out, ple_norm, w_ple_gate, b_ple_gate, w_ple,
                 final_norm):
    f = lambda a: np.asarray(a, dtype=np.float32)
    x = f(x); p = f(p)[0]
    w_in = f(w_in)[0]; w_out = f(w_out)[0]; w_pg = f(w_ple_gate)[0]; w_ple = f(w_ple)[0]
    wg = _pack_groups(w_in, w_out, w_pg)
    wal = np.zeros((D, 128), np.float32)
    for r in range(3):
        wal[:, 32 * r:32 * r + 16] = w_in[:, 2048:2064]
    walow = np.ascontiguousarray(wal.reshape(KC, 128, 128).transpose(1, 0, 2).reshape(128, KC * 128))
    wple = np.ascontiguousarray(w_ple.reshape(2, 128, D).transpose(1, 0, 2).reshape(128, 2 * D))
    pp = np.zeros((128, NPP), np.float32)
    pp[:, PP_NM:PP_NM + 16] = _vec16(f(norm_mix)[0])
    pp[:, PP_PN:PP_PN + 16] = _vec16(f(ple_norm)[0])
    pp[:, PP_FN:PP_FN + 16] = _vec16(f(final_norm))
    pp[:, PP_BPG:PP_BPG + 16] = _vec16(f(b_ple_gate)[0])
    pp[:, PP_BA:PP_BA + 4] = _vec16(f(b_alpha)[0])
    pp[:, PP_GN:PP_GN + 2] = _vec16(f(gla_norm)[0])
    pp[:, PP_CB:PP_CB + 8] = _vec16(f(conv_b)[0])
    pp[:, PP_LG:PP_LG + 8] = _vec16(f(conv_ln_g)[0])
    pp[:, PP_LB:PP_LB + 8] = _vec16(f(conv_ln_b)[0])
    cw = f(conv_w)[0]
    pp[:, PP_CW:PP_CW + 8 * 31] = cw.reshape(31, 8, 128).transpose(2, 1, 0).reshape(128, 8 * 31)
    walpha = np.ascontiguousarray(f(w_alpha)[0])
    cst = np.zeros((128, 256), np.float32)
    cst[:, 0:128] = np.eye(128, dtype=np.float32)
    cst[:, 128:256] = np.triu(np.ones((128, 128), np.float32))
    shared = {"wg": wg, "walow": walow, "wple": wple, "pp": pp, "walpha": walpha, "cst": cst}
    in_maps = []
    for b in range(NB):
        xTb = np.ascontiguousarray(x[b].T)
        pTb = np.ascontiguousarray(p[b].T)
        for hf in range(2):
            if hf == 0:
                xc = np.concatenate([np.zeros((D, 2048), np.float32), xTb[:, 0:2048]], axis=1)
            else:
                xc = xTb
            m = dict(shared)
            m["xT"] = np.ascontiguousarray(xc)
            m["pT"] = np.ascontiguousarray(pTb[:, hf * 2048:(hf + 1) * 2048])
            in_maps.append(m)
    return in_maps


def kernel(**inputs):
    in_maps = _prep_inputs(**inputs)
    nc = build_nc()
    res = run_bass_kernel_spmd(nc, in_maps, core_ids=list(range(8)))
    out = np.empty((NB, SEQ, D), np.float32)
    for b in range(NB):
        for hf in range(2):
            r = res.results[2 * b + hf]["outT"]
            out[b, hf * 2048:(hf + 1) * 2048, :] = r.T
    return out
```

```python
import numpy as np
from contextlib import ExitStack
import concourse.bass as bass
import concourse.mybir as mybir
from concourse.bass_utils import run_bass_kernel_spmd

F32 = mybir.dt.float32
BF16 = mybir.dt.bfloat16
AF = mybir.ActivationFunctionType
ALU = mybir.AluOpType

D = 2048
SEQ = 4096
NB = 4
TB = 1024
NBLK = 4
NPRE = 2
KC = 16
EPS = 1e-6
NSLOT = 6
NBIG = 20480

GID = {}
_g = 0
for h in range(4):
    GID[('q', h)] = _g; _g += 1
for h in range(4):
    GID[('k', h)] = _g; _g += 1
for h in range(4):
    for vc in range(2):
        GID[('v', h, vc)] = _g; _g += 1
for h in range(4):
    for vc in range(2):
        GID[('ga', h, vc)] = _g; _g += 1
for g in range(8):
    GID[('cv', g)] = _g; _g += 1
for g in range(8):
    GID[('cg', g)] = _g; _g += 1
for g in range(8):
    GID[('gb', g)] = _g; _g += 1
for oc in range(16):
    GID[('wo', oc)] = _g; _g += 1
for oc in range(16):
    GID[('wp', oc)] = _g; _g += 1
NG = _g

PP_NM = 0
PP_PN = 16
PP_FN = 32
PP_BPG = 48
PP_BA = 64
PP_GN = 68
PP_CB = 70
PP_LG = 78
PP_LB = 86
PP_CW = 94
NPP = 94 + 8 * 31
PN_BPG = 0
PN_BA = 16
PN_LG = 20
PN_LB = 28
NPN = 36


class Sched:
    def __init__(self, nc, es):
        self.nc = nc
        self.es = es
        self.eng = {'pe': nc.tensor, 'act': nc.scalar, 'dve': nc.vector,
                    'pool': nc.gpsimd, 'sp': nc.sync}
        self.sem = {e: es.enter_context(nc.semaphore('s_' + e))
                    for e in ('pe', 'act', 'dve', 'pool')}
        self.cnt = {e: 0 for e in self.sem}
        self.dsem = {}
        self.dcnt = {}
        self.lastw = {}
        self.readers = {}
        self.seen = {e: {} for e in self.eng}
        self.nwait = 0
        self.trace = {e: [] for e in self.eng}

    def _ln(self):
        import sys
        f = sys._getframe(1)
        out = []
        while f is not None and len(out) < 4:
            if f.f_code.co_name not in ('_ln', '_wait', 'op', 'dma', 'act', '<lambda>'):
                out.append(f.f_lineno)
            f = f.f_back
        return out

    def _deps(self, reads, writes):
        deps = {}

        def add(tok):
            if tok is None:
                return
            nm = tok[0]
            if nm not in deps or deps[nm][1] < tok[1]:
                deps[nm] = tok
        for k in reads:
            add(self.lastw.get(k))
        for k in writes:
            add(self.lastw.get(k))
            for t in self.readers.get(k, {}).values():
                add(t)
        return deps

    def _wait(self, e, deps):
        eng = self.eng[e]
        for nm, tok in deps.items():
            _, val, src, semh = tok
            if src == e and e == 'pe':
                continue
            if self.seen[e].get(nm, 0) >= val:
                continue
            eng.wait_ge(semh, val)
            self.trace[e].append(('w', nm, val, self._ln()))
            self.nwait += 1
            self.seen[e][nm] = val

    def _record(self, tok, reads, writes):
        for k in reads:
            self.readers.setdefault(k, {})[tok[0]] = tok
        for k in writes:
            self.lastw[k] = tok
            self.readers[k] = {}

    def op(self, e, fn, reads=(), writes=(), sig=True):
        reads = list(reads)
        writes = list(writes)
        deps = self._deps(reads, writes)
        self._wait(e, deps)
        ins = fn(self.eng[e])
        if sig:
            self.cnt[e] += 1
            ins.then_inc(self.sem[e], 1)
            tok = (e, self.cnt[e], e, self.sem[e])
            self.trace[e].append(('i', e, 1, self._ln()))
        else:
            tok = (e, self.cnt[e] + 1, e, self.sem[e])
        self._record(tok, reads, writes)

    def dma(self, q, fn, reads, writes, key):
        reads = list(reads)
        writes = list(writes)
        if key not in self.dsem:
            self.dsem[key] = self.es.enter_context(self.nc.semaphore('d_' + key))
            self.dcnt[key] = 0
        deps = self._deps(reads, writes)
        self._wait(q, deps)
        ins = fn(self.eng[q])
        self.dcnt[key] += 16
        ins.then_inc(self.dsem[key], 16)
        self.trace[q].append(('i', 'd_' + key, 16, self._ln()))
        tok = ('d_' + key, self.dcnt[key], 'dma', self.dsem[key])
        self._record(tok, reads, writes)

    def finish(self, e='sp'):
        eng = self.eng[e]
        for key, semh in self.dsem.items():
            if self.dcnt[key] > self.seen[e].get('d_' + key, 0):
                eng.wait_ge(semh, self.dcnt[key])


class DrySched:
    def op(self, *a, **k):
        pass

    def dma(self, *a, **k):
        pass


def run_lanes(lanes, pattern=None):
    lanes = list(lanes)
    alive = [True] * len(lanes)
    if pattern is None:
        pattern = list(range(len(lanes)))
    while any(alive):
        for i in pattern:
            if not alive[i]:
                continue
            try:
                next(lanes[i])
            except StopIteration:
                alive[i] = False


def chain(*gens):
    for g in gens:
        yield from g


def build_nc(debug=None):
    nc = bass.Bass("TRN2", target_bir_lowering=False)
    xT = nc.dram_tensor("xT", [D, 2 * 2048], F32, kind="ExternalInput").ap()
    pT = nc.dram_tensor("pT", [256, 2048], F32, kind="ExternalInput").ap()
    wg = nc.dram_tensor("wg", [NG, 128, KC * 128], F32, kind="ExternalInput").ap()
    walow = nc.dram_tensor("walow", [128, KC * 128], F32, kind="ExternalInput").ap()
    wple = nc.dram_tensor("wple", [128, 2 * D], F32, kind="ExternalInput").ap()
    ppd = nc.dram_tensor("pp", [128, NPP], F32, kind="ExternalInput").ap()
    walpha = nc.dram_tensor("walpha", [16, 512], F32, kind="ExternalInput").ap()
    cst = nc.dram_tensor("cst", [128, 256], F32, kind="ExternalInput").ap()
    outT = nc.dram_tensor("outT", [D, 2048], F32, kind="ExternalOutput").ap()

    es = ExitStack()
    with es:
        def sb(name, shape, dt):
            return es.enter_context(nc.sbuf_tensor(name, shape, dt))

        U = sb("U", [128, KC, TB], BF16)
        Y = sb("Y", [128, KC, TB], BF16)
        BIG = sb("BIG", [128, NBIG], F32)
        W = [sb("W%d" % i, [128, KC, 128], BF16) for i in range(NSLOT)]
        WPLE = sb("WPLE", [128, 2, D], BF16)
        WALOW = sb("WALOW", [128, KC, 128], BF16)
        WA3 = sb("WA3", [128, 512], BF16)
        PP = sb("PP", [128, NPP], F32)
        PN = sb("PN", [128, NPN], F32)
        CST = sb("CST", [128, 256], F32)
        IDB = sb("IDB", [128, 128], BF16)
        ONESB = sb("ONESB", [128, 128], BF16)
        ONESF = sb("ONESF", [128, 128], F32)
        RS = [sb("RSTD0", [128, TB], F32), sb("RSTD1", [128, TB], F32)]
        SQ = sb("SQ", [128, 2, TB], BF16)
        ST = sb("ST", [128, 4, 256], F32)
        HALO = sb("HALO", [128, 8, 32], BF16)
        DSS = sb("DSS", [128, 2, 8], F32)
        PBLK = sb("PBLK", [128, 2, TB], BF16)
        PS = [es.enter_context(nc.psum_tensor("ps%d" % i, [128, 512], F32)) for i in range(7)]
        PT = es.enter_context(nc.psum_tensor("pt", [128, 1024], BF16))

        IDF = CST[:, 0:128]
        MASK = CST[:, 128:256]
        real = Sched(nc, es)
        stream = []

        def bk(a, b):
            return [('big', j) for j in range(a // 256, (b - 1) // 256 + 1)]

        def vf(off, n):
            return BIG[:, off:off + n], bk(off, off + n)

        def vb(off, n_bf):
            n = n_bf // 2
            return BIG[:, off:off + n].bitcast(BF16), bk(off, off + n)

        def ppc(col):
            return PP[:, col:col + 1]

        def pnc(col):
            return PN[:, col:col + 1]

        def emit(S, dry):
            state = {'issued': 0, 'next': 0}

            def issue_w(extra_reads=()):
                i = state['issued']
                slot = i % NSLOT
                gid = GID[stream[i]]
                S.dma('pool', lambda e: e.dma_start(out=W[slot][:].rearrange("p a b -> p (a b)"),
                                                    in_=wg[gid]),
                      list(extra_reads), [('w', slot)], 'w%d' % slot)
                state['issued'] += 1

            def next_w(expect):
                if dry:
                    stream.append(expect)
                    return 0
                i = state['next']
                assert stream[i] == expect, (stream[i], expect)
                while state['issued'] < min(len(stream), i + NSLOT):
                    issue_w()
                state['next'] += 1
                return i % NSLOT

            def act(out, in_, func, reads, writes, bias=None, scale=None):
                kw = {}
                if bias is not None:
                    kw['bias'] = bias
                if scale is not None:
                    kw['scale'] = scale
                S.op('act', lambda e: e.activation(out=out, in_=in_, func=func, **kw), reads, writes)

            def sigmoid_chain(dst, dkeys, src, skeys, scale=-1.0, bias=None):
                act(dst, src, AF.Exp, skeys, dkeys, bias=bias, scale=scale)
                act(dst, dst, AF.Ln, dkeys, dkeys, bias=1.0)
                act(dst, dst, AF.Exp, dkeys, dkeys, scale=-1.0)

            def rstd_from(dst, dkeys, src, skeys, inv_n):
                act(dst, src, AF.Ln, skeys, dkeys, bias=EPS, scale=inv_n)
                act(dst, dst, AF.Exp, dkeys, dkeys, scale=-0.5)

            def proj(slot, banks, rhs_of, rkeys_of, halves=(0, 1), wap=None, M=128, N=512):
                if wap is None:
                    order = [(kc, hf) for hf in halves for kc in range(KC)]
                else:
                    order = [(kc, hf) for kc in range(KC) for hf in halves]
                for kc, hf in order:
                    if True:
                        lhsT = W[slot][:, kc, :] if wap is None else wap(kc)
                        wk = [('w', slot)] if wap is None else ['WALOW']
                        S.op('pe', lambda e: e.matmul(
                            PS[banks[hf]][0:M, 0:N], lhsT, rhs_of(kc, hf),
                            start=(kc == 0), stop=(kc == KC - 1)),
                            wk + rkeys_of(kc), [('ps', banks[hf])], sig=(kc == KC - 1))

            def u_rhs(kc, hf):
                return U[:, kc, hf * 512:(hf + 1) * 512]

            def u_keys(kc):
                return [('U', kc)]

            def y_rhs(kc, hf):
                return Y[:, kc, hf * 512:(hf + 1) * 512]

            def y_keys(kc):
                return [('Y', kc)]

            class Banks:
                def __init__(self, lst):
                    self.lst = lst
                    self.i = 0

                def one(self):
                    b = self.lst[self.i % len(self.lst)]
                    self.i += 1
                    return b

                def pair(self):
                    return (self.one(), self.one())

            if not dry:
                S.dma('sp', lambda e: e.dma_start(out=PP[:], in_=ppd), [], ['PP'], 'c0')
                S.dma('sp', lambda e: e.dma_start(out=CST[:], in_=cst), [], ['CST'], 'c1')
                WA3F = RS[1][:, 0:512]
                k_wa = [('rstd', 1, 0)]
                S.op('dve', lambda e: e.memset(WA3F, 0.0), [], k_wa)
                for r in range(3):
                    S.dma('sp', lambda e: e.dma_start(out=RS[1][32 * r:32 * r + 16, 0:512], in_=walpha),
                          [], k_wa, 'c2')
                S.op('act', lambda e: e.copy(out=WA3[0:80, :], in_=RS[1][0:80, 0:512]), k_wa, ['WA3'])
                S.op('dve', lambda e: e.tensor_tensor(out=WA3[64:80, :], in0=RS[1][64:80, 0:512],
                                                      in1=WA3[64:80, :], op=ALU.subtract), k_wa + ['WA3'], ['WA3'])
                S.dma('pool', lambda e: e.dma_start(out=WALOW[:].rearrange("p a b -> p (a b)"), in_=walow),
                      [], ['WALOW'], 'c3')
                S.op('dve', lambda e: e.memset(ONESB[:], 1.0), [], ['ONESB'])
                S.op('dve', lambda e: e.memset(ONESF[:], 1.0), [], ['ONESF'])
                S.op('dve', lambda e: e.memset(ST[:].rearrange("p a b -> p (a b)"), 0.0), [],
                     ['S0', 'S1', 'S2', 'S3'])
                S.op('dve', lambda e: e.memset(HALO[:].rearrange("p a b -> p (a b)"), 0.0), [],
                     ['halo%d' % g for g in range(8)])
                S.op('dve', lambda e: e.tensor_copy(out=IDB[:], in_=IDF), ['CST'], ['IDB'])
                S.op('act', lambda e: e.mul(out=PN[:, PN_BPG:PN_BPG + 16], in_=PP[:, PP_BPG:PP_BPG + 16],
                                            mul=-1.0), ['PP'], ['PN'])
                S.op('act', lambda e: e.mul(out=PN[:, PN_BA:PN_BA + 4], in_=PP[:, PP_BA:PP_BA + 4],
                                            mul=-1.0), ['PP'], ['PN'])

            cur = {}

            def gla(h, o, bl, main, lane):
                B_ = Banks(bl)
                if main:
                    SP, k_sp = vf(o + 0, 1024)
                    Bt, k_b = vf(o + 1024, 1024)
                    E1, k_e1 = vf(o + 2048, 1024)
                    E2, k_e2 = vf(o + 3072, 1024)
                    QD, k_qd = vb(o + 4096, 1024)
                    KD, k_kd = vb(o + 4608, 1024)
                    KD2T, k_kd2t = vb(o + 5120, 1024)
                    VT, k_vt = vb(o + 5632, 2048)
                    VTOK, k_vtok = vb(o + 6656, 2048)
                    KTOK, k_ktok = vb(o + 7680, 1024)
                    SGA, k_sga = vb(o + 8192, 2048)
                    SF, k_sf = vf(o + 9216, 1792)
                    SBf, k_sb = vb(o + 11008, 2048)
                    Rr, k_r = vf(o + 12032, 512)
                    SGT, k_sgt = SP, k_sp
                    Tt, k_t = E1, k_e1
                    STB, k_stb = vb(o + 5632, 1024)
                    SQO, k_sqo = vb(o + 6144, 1024)
                    SGA = SGA.rearrange("p (a b) -> p a b", a=2)
                    SBf = SBf.rearrange("p (a b) -> p a b", a=8)
                    SQO = SQO.rearrange("p (a b) -> p a b", a=2)
                    Tt = Tt.rearrange("p (a b) -> p a b", a=2)
                else:
                    SP, k_sp = vf(o + 0, 1024)
                    Bt, k_b = vf(o + 1024, 1024)
                    E2, k_e2 = vf(o + 2048, 1024)
                    KD2T, k_kd2t = vb(o + 3072, 1024)
                    VT, k_vt = vb(o + 3584, 2048)
                    VTOK, k_vtok = vb(o + 4608, 2048)
                    KTOK, k_ktok = vb(o + 5632, 1024)
                    SF, k_sf = vf(o + 6144, 1792)
                VT = VT.rearrange("p (a b) -> p a b", a=2)
                VTOK = VTOK.rearrange("p (a b) -> p a b", a=8)
                KTOK = KTOK.rearrange("p (a b) -> p a b", a=8)
                SF = SF.rearrange("p (a b) -> p a b", a=7)
                Sh = ST[:, h, :]
                k_s = ['S%d' % h]
                DS = DSS[:, lane, :]
                k_ds = ['DS%d' % lane]
                k_pt = ['pt']

                bz = B_.pair()
                for hf in range(2):
                    S.op('pe', lambda e: e.matmul(
                        PS[bz[hf]][:, :], WA3[0:80, h * 128:(h + 1) * 128],
                        cur['ALOW'][0:80, hf * 512:(hf + 1) * 512], start=True, stop=True),
                        ['WA3'] + cur['k_alow'], [('ps', bz[hf])])
                for hf in range(2):
                    sl = slice(hf * 512, (hf + 1) * 512)
                    ksp_h = bk(o + hf * 512, o + (hf + 1) * 512)
                    kb_h = bk(o + 1024 + hf * 512, o + 1024 + (hf + 1) * 512)
                    act(SP[:, sl], PS[bz[hf]][:, :], AF.Exp, [('ps', bz[hf]), 'PN'], ksp_h,
                        bias=pnc(PN_BA + h), scale=-1.0)
                    act(SP[:, sl], SP[:, sl], AF.Ln, ksp_h, ksp_h, bias=1.0)
                    for cc in range(4):
                        c = hf * 4 + cc
                        S.op('dve', lambda e: e.tensor_tensor_scan(
                            out=Bt[:, c * 128:(c + 1) * 128], data0=ONESF[:, :],
                            data1=SP[:, c * 128:(c + 1) * 128], initial=0.0,
                            op0=ALU.mult, op1=ALU.add), ksp_h + ['ONESF'], kb_h)
                if main:
                    act(E1[:, :], Bt[:, :], AF.Exp, k_b, k_e1, scale=-1.0 / 16.0)
                act(E2[:, :], Bt[:, :], AF.Exp, k_b, k_e2, scale=1.0 / 16.0)
                act(DS, Bt.rearrange("p (c t) -> p c t", t=128)[:, :, 127], AF.Exp,
                    k_b, k_ds, scale=-1.0 / 16.0)
                yield
                if main:
                    slot = next_w(('q', h))
                    bq = B_.pair()
                    proj(slot, bq, u_rhs, u_keys)
                    for hf in range(2):
                        sl = slice(hf * 512, (hf + 1) * 512)
                        S.op('dve', lambda e: e.scalar_tensor_tensor(
                            out=QD[:, sl], in0=PS[bq[hf]][:, :], scalar=float(128 ** -0.5),
                            in1=E1[:, sl], op0=ALU.mult, op1=ALU.mult),
                            [('ps', bq[hf])] + k_e1, k_qd)
                    yield
                slot = next_w(('k', h))
                bkk = B_.pair()
                proj(slot, bkk, u_rhs, u_keys)
                for hf in range(2):
                    sl = slice(hf * 512, (hf + 1) * 512)
                    if main:
                        S.op('dve', lambda e: e.tensor_tensor(
                            out=KD[:, sl], in0=PS[bkk[hf]][:, :], in1=E2[:, sl], op=ALU.mult),
                            [('ps', bkk[hf])] + k_e2, k_kd)
                    for cc in range(4):
                        c = hf * 4 + cc
                        S.op('dve', lambda e: e.scalar_tensor_tensor(
                            out=KD2T[:, c * 128:(c + 1) * 128],
                            in0=PS[bkk[hf]][:, cc * 128:(cc + 1) * 128],
                            scalar=DS[:, c:c + 1], in1=E2[:, c * 128:(c + 1) * 128],
                            op0=ALU.mult, op1=ALU.mult),
                            [('ps', bkk[hf])] + k_ds + k_e2, k_kd2t)
                yield
                for vc in range(2):
                    slot = next_w(('v', h, vc))
                    bv = B_.pair()
                    proj(slot, bv, u_rhs, u_keys)
                    for hf in range(2):
                        act(VT[:, vc, hf * 512:(hf + 1) * 512], PS[bv[hf]][:, :], AF.Copy,
                            [('ps', bv[hf])], k_vt)
                    yield
                for c in range(8):
                    S.op('pe', lambda e: e.transpose(
                        out=PT[:, c * 128:(c + 1) * 128], in_=KD2T[:, c * 128:(c + 1) * 128],
                        identity=IDB[:, :]), k_kd2t + ['IDB'], k_pt, sig=(c == 7))
                S.op('act', lambda e: e.copy(out=KTOK.rearrange("p a b -> p (a b)"), in_=PT[:, :]),
                     k_pt, k_ktok)
                if main:
                    slot = next_w(('ga', h, 0))
                    bg = B_.pair()
                    proj(slot, bg, u_rhs, u_keys)
                    for hf in range(2):
                        sl = slice(hf * 512, (hf + 1) * 512)
                        sigmoid_chain(SGT[:, sl], k_sgt, PS[bg[hf]][:, :], [('ps', bg[hf])])
                        S.op('dve', lambda e: e.tensor_tensor(
                            out=SGA[:, 0, sl], in0=PS[bg[hf]][:, :], in1=SGT[:, sl],
                            op=ALU.mult), [('ps', bg[hf])] + k_sgt, k_sga)
                yield
                for vc in range(2):
                    for c in range(8):
                        S.op('pe', lambda e: e.transpose(
                            out=PT[:, c * 128:(c + 1) * 128], in_=VT[:, vc, c * 128:(c + 1) * 128],
                            identity=IDB[:, :]), k_vt + ['IDB'], k_pt, sig=(c == 7))
                    S.op('dve', lambda e: e.tensor_copy(
                        out=VTOK[:, :, vc * 128:(vc + 1) * 128],
                        in_=PT[:, :].rearrange("p (a b) -> p a b", a=8)), k_pt, k_vtok)
                    if vc == 0:
                        if main:
                            slot = next_w(('ga', h, 1))
                            bg = B_.pair()
                            proj(slot, bg, u_rhs, u_keys)
                            for hf in range(2):
                                sl = slice(hf * 512, (hf + 1) * 512)
                                sigmoid_chain(SGT[:, sl], k_sgt, PS[bg[hf]][:, :], [('ps', bg[hf])])
                                S.op('dve', lambda e: e.tensor_tensor(
                                    out=SGA[:, 1, sl], in0=PS[bg[hf]][:, :], in1=SGT[:, sl],
                                    op=ALU.mult), [('ps', bg[hf])] + k_sgt, k_sga)
                        yield
                yield
                ub = [B_.one() for _ in range(len(bl))]
                nub = len(ub)
                if main:
                    act(SBf[:, 0, :], Sh, AF.Copy, k_s, k_sb)
                upfront = main and nub >= 4
                if upfront:
                    for c in range(8):
                        bnk = ub[(c // 2) % nub]
                        S.op('pe', lambda e: e.matmul(
                            PS[bnk][:, (c % 2) * 256:(c % 2 + 1) * 256], KTOK[:, c, :], VTOK[:, c, :],
                            start=True, stop=True), k_ktok + k_vtok, [('ps', bnk)])
                for c in range(8):
                    bnk = ub[(c // 2) % nub]
                    if not upfront:
                        S.op('pe', lambda e: e.matmul(
                            PS[bnk][:, (c % 2) * 256:(c % 2 + 1) * 256], KTOK[:, c, :], VTOK[:, c, :],
                            start=True, stop=True), k_ktok + k_vtok, [('ps', bnk)])
                    src = Sh if c == 0 else SF[:, c - 1, :]
                    dst = Sh if c == 7 else SF[:, c, :]
                    rk = (k_s if c == 0 else k_sf)
                    wk = (k_s if c == 7 else k_sf)
                    S.op('dve', lambda e: e.scalar_tensor_tensor(
                        out=dst, in0=src, scalar=DS[:, c:c + 1],
                        in1=PS[bnk][:, (c % 2) * 256:(c % 2 + 1) * 256],
                        op0=ALU.mult, op1=ALU.add),
                        rk + k_ds + [('ps', bnk)], wk)
                    if c % 2 == 1:
                        yield
                if not main:
                    return
                act(SBf[:, 1:8, :], SF[:, 0:7, :], AF.Copy, k_sf, k_sb)
                sbk = [B_.one(), B_.one()]
                for c in range(8):
                    S.op('pe', lambda e: e.matmul(
                        PS[sbk[c // 4]][:, (c % 4) * 128:(c % 4 + 1) * 128],
                        KD[:, c * 128:(c + 1) * 128], QD[:, c * 128:(c + 1) * 128],
                        start=True, stop=True), k_kd + k_qd, [('ps', sbk[c // 4])])
                for hf in range(2):
                    S.op('dve', lambda e: e.tensor_tensor(
                        out=STB[:, hf * 512:(hf + 1) * 512].rearrange("p (a b) -> p a b", a=4),
                        in0=PS[sbk[hf]][:, :].rearrange("p (a b) -> p a b", a=4),
                        in1=MASK.unsqueeze(1).to_broadcast([128, 4, 128]), op=ALU.mult),
                        [('ps', sbk[hf]), 'CST'], k_stb)
                yield
                for hf in range(2):
                    obk = [B_.one(), B_.one()]
                    for vc in range(2):
                        bo = obk[vc]
                        for cc in range(4):
                            c = hf * 4 + cc
                            S.op('pe', lambda e: e.matmul(
                                PS[bo][:, cc * 128:(cc + 1) * 128],
                                VTOK[:, c, vc * 128:(vc + 1) * 128], STB[:, c * 128:(c + 1) * 128],
                                start=True, stop=False), k_vtok + k_stb, [('ps', bo)], sig=False)
                            S.op('pe', lambda e: e.matmul(
                                PS[bo][:, cc * 128:(cc + 1) * 128],
                                SBf[:, c, vc * 128:(vc + 1) * 128], QD[:, c * 128:(c + 1) * 128],
                                start=False, stop=True), k_sb + k_qd, [('ps', bo)], sig=(cc == 3))
                        act(SQO[:, vc, :], PS[bo][:, :], AF.Square, [('ps', bo)], k_sqo)
                    yield
                    sbank = B_.one()
                    for vc in range(2):
                        S.op('pe', lambda e: e.matmul(
                            PS[sbank][:, :], ONESB[:, :], SQO[:, vc, :], start=(vc == 0), stop=(vc == 1)),
                            k_sqo + ['ONESB'], [('ps', sbank)], sig=True)
                    rstd_from(Rr[:, :], k_r, PS[sbank][:, :], [('ps', sbank)], 1.0 / 256.0)
                    for vc in range(2):
                        bo = obk[vc]
                        sl = slice(hf * 512, (hf + 1) * 512)
                        S.op('dve', lambda e: e.scalar_tensor_tensor(
                            out=Tt[:, vc, :], in0=PS[bo][:, :], scalar=ppc(PP_GN + vc), in1=Rr[:, :],
                            op0=ALU.mult, op1=ALU.mult), [('ps', bo), 'PP'] + k_r, k_t)
                        S.op('dve', lambda e: e.tensor_tensor(
                            out=Y[:, 2 * h + vc, sl], in0=Tt[:, vc, :], in1=SGA[:, vc, sl],
                            op=ALU.mult), k_t + k_sga, [('Y', 2 * h + vc)])
                    yield

            def conv(g, o, bl, main, first=False):
                B_ = Banks(bl)
                hk = ['halo%d' % g]
                if not main:
                    SIG, k_sig = vf(o + 0, 256)
                    Cb, k_c = vb(o + 256, 512)
                    slot = next_w(('cg', g))
                    b1 = B_.one()
                    proj(slot, (b1, b1), lambda kc, hf: U[:, kc, 896:1024], u_keys, halves=(0,), N=128)
                    sigmoid_chain(SIG[:, 0:128], k_sig, PS[b1][:, 0:128], [('ps', b1)])
                    yield
                    slot = next_w(('cv', g))
                    b2 = B_.one()
                    proj(slot, (b2, b2), lambda kc, hf: U[:, kc, 896:1024], u_keys, halves=(0,), N=128)
                    S.op('dve', lambda e: e.tensor_tensor(
                        out=Cb[:, 0:128], in0=PS[b2][:, 0:128], in1=SIG[:, 0:128], op=ALU.mult),
                        [('ps', b2)] + k_sig, k_c)
                    S.op('dve', lambda e: e.tensor_copy(out=HALO[:, g, 0:30], in_=Cb[:, 98:128]),
                         k_c, hk)
                    yield
                    return
                SIG, k_sig = vf(o + 0, 1024)
                Cb, k_c = vb(o + 1024, 1536)
                SGB, k_sgb = vb(o + 1792, 1024)
                SG2, k_sg2 = SIG, k_sig
                TAPS, k_taps = vb(o + 2304, 4096)
                TAPS = TAPS.rearrange("p (a b) -> p a b", a=32)
                XC, k_xc = vf(o + 4352, 512)
                XCB, k_xcb = vb(o + 4864, 512)
                Dd, k_d = vf(o + 5120, 512)
                SQC, k_sqc = vb(o + 5632, 512)
                RC, k_rc = vf(o + 5888, 512)
                A0, k_a0 = vf(o + 6400, 512)
                SG3, k_sg3 = vf(o + 6912, 512)
                UH = PBLK[:, :, :].rearrange("p a b -> p (a b)").rearrange("p (a b) -> p a b", a=KC)
                k_uh = [('pb', 0), ('pb', 1)]
                SIGX, k_sigx = vf(o + 4352, 128)
                slot = next_w(('cg', g))
                b1 = B_.pair()
                proj(slot, b1, u_rhs, u_keys)
                if first:
                    bx = B_.one()
                    for kc in range(KC):
                        S.op('pe', lambda e: e.matmul(PS[bx][:, 0:128], W[slot][:, kc, :], UH[:, kc, :],
                                                      start=(kc == 0), stop=(kc == KC - 1)),
                             [('w', slot)] + k_uh, [('ps', bx)], sig=(kc == KC - 1))
                    sigmoid_chain(SIGX[:, :], k_sigx, PS[bx][:, 0:128], [('ps', bx)])
                for hf in range(2):
                    sl = slice(hf * 512, (hf + 1) * 512)
                    sigmoid_chain(SIG[:, sl], k_sig, PS[b1[hf]][:, :], [('ps', b1[hf])])
                S.op('dve', lambda e: e.tensor_tensor(
                    out=TAPS[:, 0:31, :],
                    in0=IDF.unsqueeze(1).to_broadcast([128, 31, 128]),
                    in1=PP[:, PP_CW + g * 31:PP_CW + (g + 1) * 31].unsqueeze(2).to_broadcast([128, 31, 128]),
                    op=ALU.mult), ['CST', 'PP'], k_taps)
                yield
                slot = next_w(('cv', g))
                b2 = B_.pair()
                proj(slot, b2, u_rhs, u_keys)
                if first:
                    bx2 = B_.one()
                    for kc in range(KC):
                        S.op('pe', lambda e: e.matmul(PS[bx2][:, 0:128], W[slot][:, kc, :], UH[:, kc, :],
                                                      start=(kc == 0), stop=(kc == KC - 1)),
                             [('w', slot)] + k_uh, [('ps', bx2)], sig=(kc == KC - 1))
                    S.op('dve', lambda e: e.tensor_tensor(
                        out=Cb[:, 0:30], in0=PS[bx2][:, 98:128], in1=SIGX[:, 98:128], op=ALU.mult),
                        [('ps', bx2)] + k_sigx, k_c)
                else:
                    S.op('dve', lambda e: e.tensor_copy(out=Cb[:, 0:30], in_=HALO[:, g, 0:30]), hk, k_c)
                for hf in range(2):
                    sl = slice(hf * 512, (hf + 1) * 512)
                    S.op('dve', lambda e: e.tensor_tensor(
                        out=Cb[:, 30 + hf * 512:30 + (hf + 1) * 512], in0=PS[b2[hf]][:, :],
                        in1=SIG[:, sl], op=ALU.mult), [('ps', b2[hf])] + k_sig, k_c)
                S.op('dve', lambda e: e.tensor_copy(out=HALO[:, g, 0:30], in_=Cb[:, 1024:1054]), k_c, hk)
                yield
                slot = next_w(('gb', g))
                b3 = B_.pair()
                proj(slot, b3, u_rhs, u_keys)
                for hf in range(2):
                    sl = slice(hf * 512, (hf + 1) * 512)
                    sigmoid_chain(SG2[:, sl], k_sg2, PS[b3[hf]][:, :], [('ps', b3[hf])])
                    S.op('dve', lambda e: e.tensor_tensor(
                        out=SGB[:, sl], in0=PS[b3[hf]][:, :], in1=SG2[:, sl], op=ALU.mult),
                        [('ps', b3[hf])] + k_sg2, k_sgb)
                yield
                for hf in range(2):
                    sl = slice(hf * 512, (hf + 1) * 512)
                    ba = B_.one()
                    for k in range(31):
                        S.op('pe', lambda e: e.matmul(
                            PS[ba][:, :], TAPS[:, k, :], Cb[:, hf * 512 + k:hf * 512 + k + 512],
                            start=(k == 0), stop=(k == 30)), k_taps + k_c, [('ps', ba)],
                            sig=(k == 30))
                    act(XC[:, :], PS[ba][:, :], AF.Identity, [('ps', ba), 'PP'], k_xc,
                        bias=ppc(PP_CB + g))
                    act(XCB[:, :], PS[ba][:, :], AF.Identity, [('ps', ba), 'PP'], k_xcb,
                        bias=ppc(PP_CB + g))
                    yield
                    bm = B_.one()
                    S.op('pe', lambda e: e.matmul(PS[bm][:, :], ONESB[:, :], XCB[:, :],
                                                  start=True, stop=True),
                         k_xcb + ['ONESB'], [('ps', bm)])
                    S.op('dve', lambda e: e.scalar_tensor_tensor(
                        out=Dd[:, :], in0=PS[bm][:, :], scalar=-1.0 / 128.0, in1=XC[:, :],
                        op0=ALU.mult, op1=ALU.add), [('ps', bm)] + k_xc, k_d)
                    act(SQC[:, :], Dd[:, :], AF.Square, k_d, k_sqc)
                    yield
                    bv = B_.one()
                    S.op('pe', lambda e: e.matmul(PS[bv][:, :], ONESB[:, :], SQC[:, :],
                                                  start=True, stop=True),
                         k_sqc + ['ONESB'], [('ps', bv)])
                    rstd_from(RC[:, :], k_rc, PS[bv][:, :], [('ps', bv)], 1.0 / 128.0)
                    S.op('dve', lambda e: e.tensor_tensor(out=Dd[:, :], in0=Dd[:, :], in1=RC[:, :],
                                                          op=ALU.mult), k_d + k_rc, k_d)
                    act(A0[:, :], Dd[:, :], AF.Identity, k_d + ['PP'], k_a0,
                        bias=ppc(PP_LB + g), scale=ppc(PP_LG + g))
                    sigmoid_chain(SG3[:, :], k_sg3, A0[:, :], k_a0)
                    S.op('dve', lambda e: e.tensor_tensor(out=A0[:, :], in0=A0[:, :], in1=SG3[:, :],
                                                          op=ALU.mult), k_a0 + k_sg3, k_a0)
                    S.op('dve', lambda e: e.tensor_tensor(
                        out=Y[:, 8 + g, sl], in0=A0[:, :], in1=SGB[:, sl], op=ALU.mult),
                        k_a0 + k_sgb, [('Y', 8 + g)])

            XS = [vf(18432, 1024), vf(19456, 1024)]
            XSH = [vf(18432, 512), vf(18944, 512), vf(19456, 512)]
            SQN = [vb(19968, 512), vb(20224, 512)]
            PTF = PT[:, :].bitcast(F32)

            def xrows(kc):
                return slice(kc * 128, (kc + 1) * 128)

            def a1_stats(blk, bank, bkey):
                t0 = blk * TB
                R = RS[blk % 2]
                steps = [(hf, kc) for hf in range(2) for kc in range(KC)]
                n = len(steps)

                def dma_i(i):
                    hf, kc = steps[i]
                    xs, k_xs = XSH[i % 3]
                    S.dma('sp', lambda e: e.dma_start(
                        out=xs[:, :], in_=xT[xrows(kc), t0 + hf * 512:t0 + (hf + 1) * 512]),
                        [], k_xs, 'xs%d' % (i % 3))

                def sq_i(i):
                    xs, k_xs = XSH[i % 3]
                    sq, k_sq = SQN[i % 2]
                    act(sq[:, :], xs[:, :], AF.Square, k_xs, k_sq)

                def mm_i(i):
                    hf, kc = steps[i]
                    sq, k_sq = SQN[i % 2]
                    S.op('pe', lambda e: e.matmul(bank, ONESB[:, :], sq[:, :],
                                                  start=(kc == 0), stop=(kc == KC - 1)),
                         k_sq + ['ONESB'], [bkey], sig=True)
                    if kc == KC - 1:
                        rstd_from(R[:, hf * 512:(hf + 1) * 512], [('rstd', blk % 2, hf)], bank, [bkey],
                                  1.0 / D)

                dma_i(0)
                dma_i(1)
                for i in range(n + 1):
                    if i + 2 < n:
                        dma_i(i + 2)
                    if i < n:
                        sq_i(i)
                    if i >= 1:
                        mm_i(i - 1)
                    if i % 2 == 1:
                        yield

            def a1_norm(blk):
                t0 = blk * TB
                R = RS[blk % 2]

                def dma_k(kc):
                    xs, k_xs = XS[kc % 2]
                    S.dma('sp', lambda e: e.dma_start(out=xs[:, :], in_=xT[xrows(kc), t0:t0 + TB]),
                          [], k_xs, 'xn%d' % (kc % 2))

                dma_k(0)
                for kc in range(KC):
                    if kc + 1 < KC:
                        dma_k(kc + 1)
                    xs, k_xs = XS[kc % 2]
                    S.op('dve', lambda e: e.scalar_tensor_tensor(
                        out=U[:, kc, :], in0=xs[:, :], scalar=ppc(PP_NM + kc), in1=R[:, :],
                        op0=ALU.mult, op1=ALU.mult),
                        k_xs + [('rstd', blk % 2, 0), ('rstd', blk % 2, 1), 'PP'], [('U', kc)])
                    if kc % 2 == 1:
                        yield

            def a1_full(blk):
                t0 = blk * TB
                R = RS[blk % 2]
                rkk = [('rstd', blk % 2, 0), ('rstd', blk % 2, 1)]
                Xv = BIG[:, 0:KC * TB].rearrange("p (a b) -> p a b", a=KC)

                def xk(kc):
                    return bk(kc * TB, (kc + 1) * TB)
                for kc in range(KC):
                    S.dma('sp', lambda e: e.dma_start(out=Xv[:, kc, :], in_=xT[xrows(kc), t0:t0 + TB]),
                          [], xk(kc), 'x%d' % kc)
                for kc in range(KC):
                    sq = SQ[:, kc % 2, :]
                    if kc % 2 == 0:
                        act(sq, Xv[:, kc, :], AF.Square, xk(kc), [('sq', kc % 2)])
                    else:
                        S.op('dve', lambda e: e.tensor_tensor(out=sq, in0=Xv[:, kc, :], in1=Xv[:, kc, :],
                                                              op=ALU.mult), xk(kc), [('sq', kc % 2)])
                    for hf in range(2):
                        S.op('pe', lambda e: e.matmul(
                            PS[5 + hf][:, :], ONESB[:, :], sq[:, hf * 512:(hf + 1) * 512],
                            start=(kc == 0), stop=(kc == KC - 1)),
                            [('sq', kc % 2), 'ONESB'], [('ps', 5 + hf)], sig=True)
                for hf in range(2):
                    rstd_from(R[:, hf * 512:(hf + 1) * 512], [rkk[hf]], PS[5 + hf][:, :],
                              [('ps', 5 + hf)], 1.0 / D)
                for kc in range(KC):
                    S.op('dve', lambda e: e.scalar_tensor_tensor(
                        out=U[:, kc, :], in0=Xv[:, kc, :], scalar=ppc(PP_NM + kc), in1=R[:, :],
                        op0=ALU.mult, op1=ALU.mult),
                        xk(kc) + rkk + ['PP'], [('U', kc)])

            a1_full(0)
            if not dry:
                for _ in range(NSLOT):
                    issue_w(extra_reads=bk(15 * TB, 16 * TB))
            S.dma('pool', lambda e: e.dma_start(out=WPLE[:].rearrange("p a b -> p (a b)"), in_=wple),
                  [], ['WPLE'], 'c4')

            for blk in range(NBLK):
                main = blk >= NPRE
                t0 = blk * TB
                RSTD = RS[blk % 2]
                rk = [('rstd', blk % 2, 0), ('rstd', blk % 2, 1)]
                ALOW = RSTD[:, 0:512].bitcast(BF16)
                k_alow = ['alow'] + rk
                cur['ALOW'] = ALOW
                cur['k_alow'] = k_alow
                Hv = BIG[:, 0:KC * TB].rearrange("p (a b) -> p a b", a=KC)

                def xkeys(kc):
                    return bk(kc * TB, (kc + 1) * TB)

                proj(None, (0, 1), u_rhs, u_keys, wap=lambda kc: WALOW[:, kc, :])
                for hf in range(2):
                    sl = slice(hf * 512, (hf + 1) * 512)
                    act(ALOW[0:80, sl], PS[hf][0:80, :], AF.Copy, [('ps', hf)], k_alow)
                    S.op('dve', lambda e: e.tensor_tensor(out=ALOW[32:48, sl], in0=PS[hf][32:48, :],
                                                          in1=ALOW[32:48, sl], op=ALU.subtract),
                         [('ps', hf)] + k_alow, k_alow)

                if main:
                    lane1 = chain(*[gla(h, 0, [0, 1, 2, 3], True, 0) for h in range(4)])
                    fst = (blk == NPRE)
                    lane2 = chain(*[conv(g, 12544, [4, 5, 6], True, fst) for g in range(6)])
                    run_lanes([lane1, lane2])
                    run_lanes([conv(6, 0, [0, 1, 2, 3], True, fst), conv(7, 12544, [4, 5, 6], True, fst)])
                else:
                    tasksA = [gla(0, 0, [0, 1, 2, 3], False, 0), gla(2, 0, [0, 1, 2, 3], False, 0)]
                    tasksB = [gla(1, 7936, [4, 5, 6], False, 1), gla(3, 7936, [4, 5, 6], False, 1)]
                    run_lanes([chain(*tasksA), chain(*tasksB)])
                    if blk == NPRE - 1:
                        UHs = PBLK[:, :, :].rearrange("p a b -> p (a b)").rearrange("p (a b) -> p a b", a=KC)
                        S.op('act', lambda e: e.copy(out=UHs, in_=U[:, :, 896:1024]),
                             [('U', kc) for kc in range(KC)], [('pb', 0), ('pb', 1)])
                    a1_full(blk + 1)
                    continue

                tm = (blk - NPRE) * TB
                for kc2 in range(2):
                    S.dma('pool', lambda e: e.dma_start(
                        out=PBLK[:, kc2, :], in_=pT[kc2 * 128:(kc2 + 1) * 128, tm:tm + TB]),
                        [], [('pb', kc2)], 'p%d' % kc2)
                for oc in range(KC):
                    S.dma('sp', lambda e: e.dma_start(out=Hv[:, oc, :],
                                                      in_=xT[oc * 128:(oc + 1) * 128, t0:t0 + TB]),
                          [], xkeys(oc), 'x%d' % oc)

                cnts = {'n': 0}

                def stats_sq(oc):
                    sq = SQ[:, oc % 2, :]
                    act(sq, Hv[:, oc, :], AF.Square, xkeys(oc), [('sq', oc % 2)])

                def stats_mm(oc, do_sq=True):
                    cnts['n'] += 1
                    first = cnts['n'] == 1
                    last = cnts['n'] == KC
                    sq = SQ[:, oc % 2, :]
                    if do_sq:
                        stats_sq(oc)
                    for hf in range(2):
                        S.op('pe', lambda e: e.matmul(
                            PS[5 + hf][:, :], ONESB[:, :], sq[:, hf * 512:(hf + 1) * 512],
                            start=first, stop=last),
                            [('sq', oc % 2), 'ONESB'], [('ps', 5 + hf)], sig=True)

                def b1(ocs, banks):
                    pend = None
                    for oc in ocs:
                        slot = next_w(('wo', oc))
                        proj(slot, banks, y_rhs, y_keys)
                        if pend is not None:
                            stats_mm(pend)
                        yield
                        for hf in range(2):
                            sl = slice(hf * 512, (hf + 1) * 512)
                            S.op('dve', lambda e: e.tensor_tensor(
                                out=Hv[:, oc, sl], in0=Hv[:, oc, sl], in1=PS[banks[hf]][:, :], op=ALU.add),
                                xkeys(oc) + [('ps', banks[hf])], xkeys(oc))
                        pend = oc
                    yield
                    stats_mm(pend)

                cnts['n'] = 0
                lanesB = [b1(range(0, KC, 2), (0, 1)), b1(range(1, KC, 2), (2, 3))]
                nxt = blk + 1 < NBLK
                if nxt:
                    lanesB.append(a1_stats(blk + 1, PTF, 'pt'))
                run_lanes(lanesB)
                for hf in range(2):
                    rstd_from(RSTD[:, hf * 512:(hf + 1) * 512], [rk[hf]], PS[5 + hf][:, :],
                              [('ps', 5 + hf)], 1.0 / D)
                for kc in range(KC):
                    S.op('dve', lambda e: e.scalar_tensor_tensor(
                        out=Y[:, kc, :], in0=Hv[:, kc, :], scalar=ppc(PP_PN + kc), in1=RSTD[:, :],
                        op0=ALU.mult, op1=ALU.mult),
                        xkeys(kc) + rk + ['PP'], [('Y', kc)])

                def b2(ocs, banks, o, pbank):
                    G0, k_g0 = vf(o, 1024)
                    pend = None
                    for oc in ocs:
                        if pend is not None:
                            stats_sq(pend)
                        slot = next_w(('wp', oc))
                        proj(slot, banks, y_rhs, y_keys)
                        if pend is not None:
                            stats_mm(pend, do_sq=False)
                        pbs = []
                        for hf in range(2):
                            pb_ap = PS[4][:, :] if hf == 0 else PTF
                            pb_k = ('ps', 4) if hf == 0 else 'pt'
                            pbs.append((pb_ap, pb_k))
                            for kc2 in range(2):
                                S.op('pe', lambda e: e.matmul(
                                    pb_ap, WPLE[:, kc2, oc * 128:(oc + 1) * 128],
                                    PBLK[:, kc2, hf * 512:(hf + 1) * 512], start=(kc2 == 0), stop=(kc2 == 1)),
                                    ['WPLE', ('pb', kc2)], [pb_k], sig=(kc2 == 1))
                        for hf in range(2):
                            sl = slice(hf * 512, (hf + 1) * 512)
                            sigmoid_chain(G0[:, sl], k_g0, PS[banks[hf]][:, :], [('ps', banks[hf]), 'PN'],
                                          scale=-1.0, bias=pnc(PN_BPG + oc))
                        for hf in range(2):
                            sl = slice(hf * 512, (hf + 1) * 512)
                            pb_ap, pb_k = pbs[hf]
                            S.op('dve', lambda e: e.tensor_tensor(
                                out=G0[:, sl], in0=G0[:, sl], in1=pb_ap, op=ALU.mult),
                                k_g0 + [pb_k], k_g0)
                            S.op('dve', lambda e: e.tensor_tensor(
                                out=Hv[:, oc, sl], in0=Hv[:, oc, sl], in1=G0[:, sl], op=ALU.add),
                                xkeys(oc) + k_g0, xkeys(oc))
                        pend = oc
                        yield
                    stats_mm(pend)

                cnts['n'] = 0
                lanesB = [b2(range(0, KC, 2), (0, 1), 16384, 4), b2(range(1, KC, 2), (2, 3), 17408, 4)]
                if nxt:
                    lanesB.append(a1_norm(blk + 1))
                run_lanes(lanesB)
                for hf in range(2):
                    rstd_from(RSTD[:, hf * 512:(hf + 1) * 512], [rk[hf]], PS[5 + hf][:, :],
                              [('ps', 5 + hf)], 1.0 / D)
                for oc in range(KC):
                    S.op('dve', lambda e: e.scalar_tensor_tensor(
                        out=Hv[:, oc, :], in0=Hv[:, oc, :], scalar=ppc(PP_FN + oc), in1=RSTD[:, :],
                        op0=ALU.mult, op1=ALU.mult),
                        xkeys(oc) + rk + ['PP'], xkeys(oc))
                    S.dma('sp', lambda e: e.dma_start(
                        out=outT[oc * 128:(oc + 1) * 128, tm:tm + TB], in_=Hv[:, oc, :]),
                        xkeys(oc), [], 'o%d' % oc)
            if not dry:
                assert state['next'] == len(stream), (state['next'], len(stream))

        emit(DrySched(), True)
        emit(real, False)
        real.finish('sp')
        build_nc.last_sched = real
    return nc


def _pack_groups(w_in, w_out, w_pg):
    wg = np.empty((NG, 128, KC * 128), np.float32)

    def put(gid, cols):
        wg[gid] = cols.reshape(KC, 128, 128).transpose(1, 0, 2).reshape(128, KC * 128)
    for h in range(4):
        put(GID[('q', h)], w_in[:, h * 128:(h + 1) * 128])
        put(GID[('k', h)], w_in[:, 512 + h * 128:512 + (h + 1) * 128])
        for vc in range(2):
            c0 = 1024 + h * 256 + vc * 128
            put(GID[('v', h, vc)], w_in[:, c0:c0 + 128])
            c0 = 2064 + h * 256 + vc * 128
            put(GID[('ga', h, vc)], w_in[:, c0:c0 + 128])
    for g in range(8):
        put(GID[('cv', g)], w_in[:, 3088 + g * 128:3088 + (g + 1) * 128])
        put(GID[('cg', g)], w_in[:, 4112 + g * 128:4112 + (g + 1) * 128])
        put(GID[('gb', g)], w_in[:, 5136 + g * 128:5136 + (g + 1) * 128])
    for oc in range(16):
        put(GID[('wo', oc)], w_out[:, oc * 128:(oc + 1) * 128])
        put(GID[('wp', oc)], w_pg[:, oc * 128:(oc + 1) * 128])
    return wg


def _vec16(v):
    return np.ascontiguousarray(v.reshape(-1, 128).T)


def _prep_inputs(x, p, norm_mix, w_in, w_alpha, b_alpha, gla_norm, conv_w, conv_b,
                 conv_ln_g, conv_ln_b, w_out, ple_norm, w_ple_gate, b_ple_gate, w_ple,
                 final_norm):
    f = lambda a: np.asarray(a, dtype=np.float32)
    x = f(x); p = f(p)[0]
    w_in = f(w_in)[0]; w_out = f(w_out)[0]; w_pg = f(w_ple_gate)[0]; w_ple = f(w_ple)[0]
    wg = _pack_groups(w_in, w_out, w_pg)
    wal = np.zeros((D, 128), np.float32)
    for r in range(3):
        wal[:, 32 * r:32 * r + 16] = w_in[:, 2048:2064]
    walow = np.ascontiguousarray(wal.reshape(KC, 128, 128).transpose(1, 0, 2).reshape(128, KC * 128))
    wple = np.ascontiguousarray(w_ple.reshape(2, 128, D).transpose(1, 0, 2).reshape(128, 2 * D))
    pp = np.zeros((128, NPP), np.float32)
    pp[:, PP_NM:PP_NM + 16] = _vec16(f(norm_mix)[0])
    pp[:, PP_PN:PP_PN + 16] = _vec16(f(ple_norm)[0])
    pp[:, PP_FN:PP_FN + 16] = _vec16(f(final_norm))
    pp[:, PP_BPG:PP_BPG + 16] = _vec16(f(b_ple_gate)[0])
    pp[:, PP_BA:PP_BA + 4] = _vec16(f(b_alpha)[0])
    pp[:, PP_GN:PP_GN + 2] = _vec16(f(gla_norm)[0])
    pp[:, PP_CB:PP_CB + 8] = _vec16(f(conv_b)[0])
    pp[:, PP_LG:PP_LG + 8] = _vec16(f(conv_ln_g)[0])
    pp[:, PP_LB:PP_LB + 8] = _vec16(f(conv_ln_b)[0])
    cw = f(conv_w)[0]
    pp[:, PP_CW:PP_CW + 8 * 31] = cw.reshape(31, 8, 128).transpose(2, 1, 0).reshape(128, 8 * 31)
    walpha = np.ascontiguousarray(f(w_alpha)[0])
    cst = np.zeros((128, 256), np.float32)
    cst[:, 0:128] = np.eye(128, dtype=np.float32)
    cst[:, 128:256] = np.triu(np.ones((128, 128), np.float32))
    shared = {"wg": wg, "walow": walow, "wple": wple, "pp": pp, "walpha": walpha, "cst": cst}
    in_maps = []
    for b in range(NB):
        xTb = np.ascontiguousarray(x[b].T)
        pTb = np.ascontiguousarray(p[b].T)
        for hf in range(2):
            if hf == 0:
                xc = np.concatenate([np.zeros((D, 2048), np.float32), xTb[:, 0:2048]], axis=1)
            else:
                xc = xTb
            m = dict(shared)
            m["xT"] = np.ascontiguousarray(xc)
            m["pT"] = np.ascontiguousarray(pTb[:, hf * 2048:(hf + 1) * 2048])
            in_maps.append(m)
    return in_maps


def kernel(**inputs):
    in_maps = _prep_inputs(**inputs)
    nc = build_nc()
    res = run_bass_kernel_spmd(nc, in_maps, core_ids=list(range(8)))
    out = np.empty((NB, SEQ, D), np.float32)
    for b in range(NB):
        for hf in range(2):
            r = res.results[2 * b + hf]["outT"]
            out[b, hf * 2048:(hf + 1) * 2048, :] = r.T
    return out
```

```python
import numpy as np
from contextlib import ExitStack
import concourse.bass as bass
import concourse.mybir as mybir
from concourse.bass_utils import run_bass_kernel_spmd

F32 = mybir.dt.float32
BF16 = mybir.dt.bfloat16
AF = mybir.ActivationFunctionType
ALU = mybir.AluOpType

D = 2048
SEQ = 4096
NB = 4
TB = 1024
NBLK = 4
NPRE = 2
KC = 16
EPS = 1e-6
NSLOT = 6
NBIG = 20480

GID = {}
_g = 0
for h in range(4):
    GID[('q', h)] = _g; _g += 1
for h in range(4):
    GID[('k', h)] = _g; _g += 1
for h in range(4):
    for vc in range(2):
        GID[('v', h, vc)] = _g; _g += 1
for h in range(4):
    for vc in range(2):
        GID[('ga', h, vc)] = _g; _g += 1
for g in range(8):
    GID[('cv', g)] = _g; _g += 1
for g in range(8):
    GID[('cg', g)] = _g; _g += 1
for g in range(8):
    GID[('gb', g)] = _g; _g += 1
for oc in range(16):
    GID[('wo', oc)] = _g; _g += 1
for oc in range(16):
    GID[('wp', oc)] = _g; _g += 1
NG = _g

PP_NM = 0
PP_PN = 16
PP_FN = 32
PP_BPG = 48
PP_BA = 64
PP_GN = 68
PP_CB = 70
PP_LG = 78
PP_LB = 86
PP_CW = 94
NPP = 94 + 8 * 31
PN_BPG = 0
PN_BA = 16
PN_LG = 20
PN_LB = 28
NPN = 36


class Sched:
    def __init__(self, nc, es):
        self.nc = nc
        self.es = es
        self.eng = {'pe': nc.tensor, 'act': nc.scalar, 'dve': nc.vector,
                    'pool': nc.gpsimd, 'sp': nc.sync}
        self.sem = {e: es.enter_context(nc.semaphore('s_' + e))
                    for e in ('pe', 'act', 'dve', 'pool')}
        self.cnt = {e: 0 for e in self.sem}
        self.dsem = {}
        self.dcnt = {}
        self.lastw = {}
        self.readers = {}
        self.seen = {e: {} for e in self.eng}
        self.nwait = 0
        self.trace = {e: [] for e in self.eng}

    def _ln(self):
        import sys
        f = sys._getframe(1)
        out = []
        while f is not None and len(out) < 4:
            if f.f_code.co_name not in ('_ln', '_wait', 'op', 'dma', 'act', '<lambda>'):
                out.append(f.f_lineno)
            f = f.f_back
        return out

    def _deps(self, reads, writes):
        deps = {}

        def add(tok):
            if tok is None:
                return
            nm = tok[0]
            if nm not in deps or deps[nm][1] < tok[1]:
                deps[nm] = tok
        for k in reads:
            add(self.lastw.get(k))
        for k in writes:
            add(self.lastw.get(k))
            for t in self.readers.get(k, {}).values():
                add(t)
        return deps

    def _wait(self, e, deps):
        eng = self.eng[e]
        for nm, tok in deps.items():
            _, val, src, semh = tok
            if src == e and e == 'pe':
                continue
            if self.seen[e].get(nm, 0) >= val:
                continue
            eng.wait_ge(semh, val)
            self.trace[e].append(('w', nm, val, self._ln()))
            self.nwait += 1
            self.seen[e][nm] = val

    def _record(self, tok, reads, writes):
        for k in reads:
            self.readers.setdefault(k, {})[tok[0]] = tok
        for k in writes:
            self.lastw[k] = tok
            self.readers[k] = {}

    def op(self, e, fn, reads=(), writes=(), sig=True):
        reads = list(reads)
        writes = list(writes)
        deps = self._deps(reads, writes)
        self._wait(e, deps)
        ins = fn(self.eng[e])
        if sig:
            self.cnt[e] += 1
            ins.then_inc(self.sem[e], 1)
            tok = (e, self.cnt[e], e, self.sem[e])
            self.trace[e].append(('i', e, 1, self._ln()))
        else:
            tok = (e, self.cnt[e] + 1, e, self.sem[e])
        self._record(tok, reads, writes)

    def dma(self, q, fn, reads, writes, key):
        reads = list(reads)
        writes = list(writes)
        if key not in self.dsem:
            self.dsem[key] = self.es.enter_context(self.nc.semaphore('d_' + key))
            self.dcnt[key] = 0
        deps = self._deps(reads, writes)
        self._wait(q, deps)
        ins = fn(self.eng[q])
        self.dcnt[key] += 16
        ins.then_inc(self.dsem[key], 16)
        self.trace[q].append(('i', 'd_' + key, 16, self._ln()))
        tok = ('d_' + key, self.dcnt[key], 'dma', self.dsem[key])
        self._record(tok, reads, writes)

    def finish(self, e='sp'):
        eng = self.eng[e]
        for key, semh in self.dsem.items():
            if self.dcnt[key] > self.seen[e].get('d_' + key, 0):
                eng.wait_ge(semh, self.dcnt[key])


class DrySched:
    def op(self, *a, **k):
        pass

    def dma(self, *a, **k):
        pass


def run_lanes(lanes, pattern=None):
    lanes = list(lanes)
    alive = [True] * len(lanes)
    if pattern is None:
        pattern = list(range(len(lanes)))
    while any(alive):
        for i in pattern:
            if not alive[i]:
                continue
            try:
                next(lanes[i])
            except StopIteration:
                alive[i] = False


def chain(*gens):
    for g in gens:
        yield from g


def build_nc(debug=None):
    nc = bass.Bass("TRN2", target_bir_lowering=False)
    xT = nc.dram_tensor("xT", [D, 2 * 2048], F32, kind="ExternalInput").ap()
    pT = nc.dram_tensor("pT", [256, 2048], F32, kind="ExternalInput").ap()
    wg = nc.dram_tensor("wg", [NG, 128, KC * 128], F32, kind="ExternalInput").ap()
    walow = nc.dram_tensor("walow", [128, KC * 128], F32, kind="ExternalInput").ap()
    wple = nc.dram_tensor("wple", [128, 2 * D], F32, kind="ExternalInput").ap()
    ppd = nc.dram_tensor("pp", [128, NPP], F32, kind="ExternalInput").ap()
    walpha = nc.dram_tensor("walpha", [16, 512], F32, kind="ExternalInput").ap()
    cst = nc.dram_tensor("cst", [128, 256], F32, kind="ExternalInput").ap()
    outT = nc.dram_tensor("outT", [D, 2048], F32, kind="ExternalOutput").ap()

    es = ExitStack()
    with es:
        def sb(name, shape, dt):
            return es.enter_context(nc.sbuf_tensor(name, shape, dt))

        U = sb("U", [128, KC, TB], BF16)
        Y = sb("Y", [128, KC, TB], BF16)
        BIG = sb("BIG", [128, NBIG], F32)
        W = [sb("W%d" % i, [128, KC, 128], BF16) for i in range(NSLOT)]
        WPLE = sb("WPLE", [128, 2, D], BF16)
        WALOW = sb("WALOW", [128, KC, 128], BF16)
        WA3 = sb("WA3", [128, 512], BF16)
        PP = sb("PP", [128, NPP], F32)
        PN = sb("PN", [128, NPN], F32)
        CST = sb("CST", [128, 256], F32)
        IDB = sb("IDB", [128, 128], BF16)
        ONESB = sb("ONESB", [128, 128], BF16)
        ONESF = sb("ONESF", [128, 128], F32)
        RS = [sb("RSTD0", [128, TB], F32), sb("RSTD1", [128, TB], F32)]
        SQ = sb("SQ", [128, 2, TB], BF16)
        ST = sb("ST", [128, 4, 256], F32)
        HALO = sb("HALO", [128, 8, 32], BF16)
        DSS = sb("DSS", [128, 2, 8], F32)
        PBLK = sb("PBLK", [128, 2, TB], BF16)
        PS = [es.enter_context(nc.psum_tensor("ps%d" % i, [128, 512], F32)) for i in range(7)]
        PT = es.enter_context(nc.psum_tensor("pt", [128, 1024], BF16))

        IDF = CST[:, 0:128]
        MASK = CST[:, 128:256]
        real = Sched(nc, es)
        stream = []

        def bk(a, b):
            return [('big', j) for j in range(a // 256, (b - 1) // 256 + 1)]

        def vf(off, n):
            return BIG[:, off:off + n], bk(off, off + n)

        def vb(off, n_bf):
            n = n_bf // 2
            return BIG[:, off:off + n].bitcast(BF16), bk(off, off + n)

        def ppc(col):
            return PP[:, col:col + 1]

        def pnc(col):
            return PN[:, col:col + 1]

        def emit(S, dry):
            state = {'issued': 0, 'next': 0}

            def issue_w(extra_reads=()):
                i = state['issued']
                slot = i % NSLOT
                gid = GID[stream[i]]
                S.dma('pool', lambda e: e.dma_start(out=W[slot][:].rearrange("p a b -> p (a b)"),
                                                    in_=wg[gid]),
                      list(extra_reads), [('w', slot)], 'w%d' % slot)
                state['issued'] += 1

            def next_w(expect):
                if dry:
                    stream.append(expect)
                    return 0
                i = state['next']
                assert stream[i] == expect, (stream[i], expect)
                while state['issued'] < min(len(stream), i + NSLOT):
                    issue_w()
                state['next'] += 1
                return i % NSLOT

            def act(out, in_, func, reads, writes, bias=None, scale=None):
                kw = {}
                if bias is not None:
                    kw['bias'] = bias
                if scale is not None:
                    kw['scale'] = scale
                S.op('act', lambda e: e.activation(out=out, in_=in_, func=func, **kw), reads, writes)

            def sigmoid_chain(dst, dkeys, src, skeys, scale=-1.0, bias=None):
                act(dst, src, AF.Exp, skeys, dkeys, bias=bias, scale=scale)
                act(dst, dst, AF.Ln, dkeys, dkeys, bias=1.0)
                act(dst, dst, AF.Exp, dkeys, dkeys, scale=-1.0)

            def rstd_from(dst, dkeys, src, skeys, inv_n):
                act(dst, src, AF.Ln, skeys, dkeys, bias=EPS, scale=inv_n)
                act(dst, dst, AF.Exp, dkeys, dkeys, scale=-0.5)

            def proj(slot, banks, rhs_of, rkeys_of, halves=(0, 1), wap=None, M=128, N=512):
                if wap is None:
                    order = [(kc, hf) for hf in halves for kc in range(KC)]
                else:
                    order = [(kc, hf) for kc in range(KC) for hf in halves]
                for kc, hf in order:
                    if True:
                        lhsT = W[slot][:, kc, :] if wap is None else wap(kc)
                        wk = [('w', slot)] if wap is None else ['WALOW']
                        S.op('pe', lambda e: e.matmul(
                            PS[banks[hf]][0:M, 0:N], lhsT, rhs_of(kc, hf),
                            start=(kc == 0), stop=(kc == KC - 1)),
                            wk + rkeys_of(kc), [('ps', banks[hf])], sig=(kc == KC - 1))

            def u_rhs(kc, hf):
                return U[:, kc, hf * 512:(hf + 1) * 512]

            def u_keys(kc):
                return [('U', kc)]

            def y_rhs(kc, hf):
                return Y[:, kc, hf * 512:(hf + 1) * 512]

            def y_keys(kc):
                return [('Y', kc)]

            class Banks:
                def __init__(self, lst):
                    self.lst = lst
                    self.i = 0

                def one(self):
                    b = self.lst[self.i % len(self.lst)]
                    self.i += 1
                    return b

                def pair(self):
                    return (self.one(), self.one())

            if not dry:
                S.dma('sp', lambda e: e.dma_start(out=PP[:], in_=ppd), [], ['PP'], 'c0')
                S.dma('sp', lambda e: e.dma_start(out=CST[:], in_=cst), [], ['CST'], 'c1')
                WA3F = RS[1][:, 0:512]
                k_wa = [('rstd', 1, 0)]
                S.op('dve', lambda e: e.memset(WA3F, 0.0), [], k_wa)
                for r in range(3):
                    S.dma('sp', lambda e: e.dma_start(out=RS[1][32 * r:32 * r + 16, 0:512], in_=walpha),
                          [], k_wa, 'c2')
                S.op('act', lambda e: e.copy(out=WA3[0:80, :], in_=RS[1][0:80, 0:512]), k_wa, ['WA3'])
                S.op('dve', lambda e: e.tensor_tensor(out=WA3[64:80, :], in0=RS[1][64:80, 0:512],
                                                      in1=WA3[64:80, :], op=ALU.subtract), k_wa + ['WA3'], ['WA3'])
                S.dma('pool', lambda e: e.dma_start(out=WALOW[:].rearrange("p a b -> p (a b)"), in_=walow),
                      [], ['WALOW'], 'c3')
                S.op('dve', lambda e: e.memset(ONESB[:], 1.0), [], ['ONESB'])
                S.op('dve', lambda e: e.memset(ONESF[:], 1.0), [], ['ONESF'])
                S.op('dve', lambda e: e.memset(ST[:].rearrange("p a b -> p (a b)"), 0.0), [],
                     ['S0', 'S1', 'S2', 'S3'])
                S.op('dve', lambda e: e.memset(HALO[:].rearrange("p a b -> p (a b)"), 0.0), [],
                     ['halo%d' % g for g in range(8)])
                S.op('dve', lambda e: e.tensor_copy(out=IDB[:], in_=IDF), ['CST'], ['IDB'])
                S.op('act', lambda e: e.mul(out=PN[:, PN_BPG:PN_BPG + 16], in_=PP[:, PP_BPG:PP_BPG + 16],
                                            mul=-1.0), ['PP'], ['PN'])
                S.op('act', lambda e: e.mul(out=PN[:, PN_BA:PN_BA + 4], in_=PP[:, PP_BA:PP_BA + 4],
                                            mul=-1.0), ['PP'], ['PN'])

            cur = {}

            def gla(h, o, bl, main, lane):
                B_ = Banks(bl)
                if main:
                    SP, k_sp = vf(o + 0, 1024)
                    Bt, k_b = vf(o + 1024, 1024)
                    E1, k_e1 = vf(o + 2048, 1024)
                    E2, k_e2 = vf(o + 3072, 1024)
                    QD, k_qd = vb(o + 4096, 1024)
                    KD, k_kd = vb(o + 4608, 1024)
                    KD2T, k_kd2t = vb(o + 5120, 1024)
                    VT, k_vt = vb(o + 5632, 2048)
                    VTOK, k_vtok = vb(o + 6656, 2048)
                    KTOK, k_ktok = vb(o + 7680, 1024)
                    SGA, k_sga = vb(o + 8192, 2048)
                    SF, k_sf = vf(o + 9216, 1792)
                    SBf, k_sb = vb(o + 11008, 2048)
                    Rr, k_r = vf(o + 12032, 512)
                    SGT, k_sgt = SP, k_sp
                    Tt, k_t = E1, k_e1
                    STB, k_stb = vb(o + 5632, 1024)
                    SQO, k_sqo = vb(o + 6144, 1024)
                    SGA = SGA.rearrange("p (a b) -> p a b", a=2)
                    SBf = SBf.rearrange("p (a b) -> p a b", a=8)
                    SQO = SQO.rearrange("p (a b) -> p a b", a=2)
                    Tt = Tt.rearrange("p (a b) -> p a b", a=2)
                else:
                    SP, k_sp = vf(o + 0, 1024)
                    Bt, k_b = vf(o + 1024, 1024)
                    E2, k_e2 = vf(o + 2048, 1024)
                    KD2T, k_kd2t = vb(o + 3072, 1024)
                    VT, k_vt = vb(o + 3584, 2048)
                    VTOK, k_vtok = vb(o + 4608, 2048)
                    KTOK, k_ktok = vb(o + 5632, 1024)
                    SF, k_sf = vf(o + 6144, 1792)
                VT = VT.rearrange("p (a b) -> p a b", a=2)
                VTOK = VTOK.rearrange("p (a b) -> p a b", a=8)
                KTOK = KTOK.rearrange("p (a b) -> p a b", a=8)
                SF = SF.rearrange("p (a b) -> p a b", a=7)
                Sh = ST[:, h, :]
                k_s = ['S%d' % h]
                DS = DSS[:, lane, :]
                k_ds = ['DS%d' % lane]
                k_pt = ['pt']

                bz = B_.pair()
                for hf in range(2):
                    S.op('pe', lambda e: e.matmul(
                        PS[bz[hf]][:, :], WA3[0:80, h * 128:(h + 1) * 128],
                        cur['ALOW'][0:80, hf * 512:(hf + 1) * 512], start=True, stop=True),
                        ['WA3'] + cur['k_alow'], [('ps', bz[hf])])
                for hf in range(2):
                    sl = slice(hf * 512, (hf + 1) * 512)
                    ksp_h = bk(o + hf * 512, o + (hf + 1) * 512)
                    kb_h = bk(o + 1024 + hf * 512, o + 1024 + (hf + 1) * 512)
                    act(SP[:, sl], PS[bz[hf]][:, :], AF.Exp, [('ps', bz[hf]), 'PN'], ksp_h,
                        bias=pnc(PN_BA + h), scale=-1.0)
                    act(SP[:, sl], SP[:, sl], AF.Ln, ksp_h, ksp_h, bias=1.0)
                    for cc in range(4):
                        c = hf * 4 + cc
                        S.op('dve', lambda e: e.tensor_tensor_scan(
                            out=Bt[:, c * 128:(c + 1) * 128], data0=ONESF[:, :],
                            data1=SP[:, c * 128:(c + 1) * 128], initial=0.0,
                            op0=ALU.mult, op1=ALU.add), ksp_h + ['ONESF'], kb_h)
                oe1 = o + 2048
                oe2 = o + 3072 if main else o + 2048
                ke1h = [bk(oe1 + hf * 512, oe1 + (hf + 1) * 512) for hf in range(2)]
                ke2h = [bk(oe2 + hf * 512, oe2 + (hf + 1) * 512) for hf in range(2)]
                for hf in range(2):
                    sl = slice(hf * 512, (hf + 1) * 512)
                    kb_h = bk(o + 1024 + hf * 512, o + 1024 + (hf + 1) * 512)
                    if main:
                        act(E1[:, sl], Bt[:, sl], AF.Exp, kb_h, ke1h[hf], scale=-1.0 / 16.0)
                    act(E2[:, sl], Bt[:, sl], AF.Exp, kb_h, ke2h[hf], scale=1.0 / 16.0)
                act(DS, Bt.rearrange("p (c t) -> p c t", t=128)[:, :, 127], AF.Exp,
                    k_b, k_ds, scale=-1.0 / 16.0)
                yield
                if main:
                    slot = next_w(('q', h))
                    bq = B_.pair()
                    proj(slot, bq, u_rhs, u_keys)
                    for hf in range(2):
                        sl = slice(hf * 512, (hf + 1) * 512)
                        S.op('dve', lambda e: e.scalar_tensor_tensor(
                            out=QD[:, sl], in0=PS[bq[hf]][:, :], scalar=float(128 ** -0.5),
                            in1=E1[:, sl], op0=ALU.mult, op1=ALU.mult),
                            [('ps', bq[hf])] + ke1h[hf], k_qd)
                    yield
                slot = next_w(('k', h))
                bkk = B_.pair()
                proj(slot, bkk, u_rhs, u_keys)
                for hf in range(2):
                    sl = slice(hf * 512, (hf + 1) * 512)
                    if main:
                        S.op('dve', lambda e: e.tensor_tensor(
                            out=KD[:, sl], in0=PS[bkk[hf]][:, :], in1=E2[:, sl], op=ALU.mult),
                            [('ps', bkk[hf])] + ke2h[hf], k_kd)
                    for cc in range(4):
                        c = hf * 4 + cc
                        S.op('dve', lambda e: e.scalar_tensor_tensor(
                            out=KD2T[:, c * 128:(c + 1) * 128],
                            in0=PS[bkk[hf]][:, cc * 128:(cc + 1) * 128],
                            scalar=DS[:, c:c + 1], in1=E2[:, c * 128:(c + 1) * 128],
                            op0=ALU.mult, op1=ALU.mult),
                            [('ps', bkk[hf])] + k_ds + ke2h[hf], k_kd2t)
                yield
                for vc in range(2):
                    slot = next_w(('v', h, vc))
                    bv = B_.pair()
                    proj(slot, bv, u_rhs, u_keys)
                    for hf in range(2):
                        act(VT[:, vc, hf * 512:(hf + 1) * 512], PS[bv[hf]][:, :], AF.Copy,
                            [('ps', bv[hf])], k_vt)
                    yield
                for c in range(8):
                    S.op('pe', lambda e: e.transpose(
                        out=PT[:, c * 128:(c + 1) * 128], in_=KD2T[:, c * 128:(c + 1) * 128],
                        identity=IDB[:, :]), k_kd2t + ['IDB'], k_pt, sig=(c == 7))
                S.op('act', lambda e: e.copy(out=KTOK.rearrange("p a b -> p (a b)"), in_=PT[:, :]),
                     k_pt, k_ktok)
                if main:
                    slot = next_w(('ga', h, 0))
                    bg = B_.pair()
                    proj(slot, bg, u_rhs, u_keys)
                    for hf in range(2):
                        sl = slice(hf * 512, (hf + 1) * 512)
                        sigmoid_chain(SGT[:, sl], k_sgt, PS[bg[hf]][:, :], [('ps', bg[hf])])
                        S.op('dve', lambda e: e.tensor_tensor(
                            out=SGA[:, 0, sl], in0=PS[bg[hf]][:, :], in1=SGT[:, sl],
                            op=ALU.mult), [('ps', bg[hf])] + k_sgt, k_sga)
                yield
                for vc in range(2):
                    for c in range(8):
                        S.op('pe', lambda e: e.transpose(
                            out=PT[:, c * 128:(c + 1) * 128], in_=VT[:, vc, c * 128:(c + 1) * 128],
                            identity=IDB[:, :]), k_vt + ['IDB'], k_pt, sig=(c == 7))
                    S.op('dve', lambda e: e.tensor_copy(
                        out=VTOK[:, :, vc * 128:(vc + 1) * 128],
                        in_=PT[:, :].rearrange("p (a b) -> p a b", a=8)), k_pt, k_vtok)
                    if vc == 0:
                        if main:
                            slot = next_w(('ga', h, 1))
                            bg = B_.pair()
                            proj(slot, bg, u_rhs, u_keys)
                            for hf in range(2):
                                sl = slice(hf * 512, (hf + 1) * 512)
                                sigmoid_chain(SGT[:, sl], k_sgt, PS[bg[hf]][:, :], [('ps', bg[hf])])
                                S.op('dve', lambda e: e.tensor_tensor(
                                    out=SGA[:, 1, sl], in0=PS[bg[hf]][:, :], in1=SGT[:, sl],
                                    op=ALU.mult), [('ps', bg[hf])] + k_sgt, k_sga)
                        yield
                yield
                ub = [B_.one() for _ in range(len(bl))]
                nub = len(ub)
                if main:
                    act(SBf[:, 0, :], Sh, AF.Copy, k_s, k_sb)
                upfront = main and nub >= 4
                if upfront:
                    for c in range(8):
                        bnk = ub[(c // 2) % nub]
                        S.op('pe', lambda e: e.matmul(
                            PS[bnk][:, (c % 2) * 256:(c % 2 + 1) * 256], KTOK[:, c, :], VTOK[:, c, :],
                            start=True, stop=True), k_ktok + k_vtok, [('ps', bnk)])
                for c in range(8):
                    bnk = ub[(c // 2) % nub]
                    if not upfront:
                        S.op('pe', lambda e: e.matmul(
                            PS[bnk][:, (c % 2) * 256:(c % 2 + 1) * 256], KTOK[:, c, :], VTOK[:, c, :],
                            start=True, stop=True), k_ktok + k_vtok, [('ps', bnk)])
                    src = Sh if c == 0 else SF[:, c - 1, :]
                    dst = Sh if c == 7 else SF[:, c, :]
                    rk = (k_s if c == 0 else k_sf)
                    wk = (k_s if c == 7 else k_sf)
                    S.op('dve', lambda e: e.scalar_tensor_tensor(
                        out=dst, in0=src, scalar=DS[:, c:c + 1],
                        in1=PS[bnk][:, (c % 2) * 256:(c % 2 + 1) * 256],
                        op0=ALU.mult, op1=ALU.add),
                        rk + k_ds + [('ps', bnk)], wk)
                    if c % 2 == 1:
                        yield
                if not main:
                    return
                act(SBf[:, 1:8, :], SF[:, 0:7, :], AF.Copy, k_sf, k_sb)
                sbk = [B_.one(), B_.one()]
                for c in range(8):
                    S.op('pe', lambda e: e.matmul(
                        PS[sbk[c // 4]][:, (c % 4) * 128:(c % 4 + 1) * 128],
                        KD[:, c * 128:(c + 1) * 128], QD[:, c * 128:(c + 1) * 128],
                        start=True, stop=True), k_kd + k_qd, [('ps', sbk[c // 4])])
                for hf in range(2):
                    S.op('dve', lambda e: e.tensor_tensor(
                        out=STB[:, hf * 512:(hf + 1) * 512].rearrange("p (a b) -> p a b", a=4),
                        in0=PS[sbk[hf]][:, :].rearrange("p (a b) -> p a b", a=4),
                        in1=MASK.unsqueeze(1).to_broadcast([128, 4, 128]), op=ALU.mult),
                        [('ps', sbk[hf]), 'CST'], k_stb)
                yield
                for hf in range(2):
                    obk = [B_.one(), B_.one()]
                    for vc in range(2):
                        bo = obk[vc]
                        for cc in range(4):
                            c = hf * 4 + cc
                            S.op('pe', lambda e: e.matmul(
                                PS[bo][:, cc * 128:(cc + 1) * 128],
                                VTOK[:, c, vc * 128:(vc + 1) * 128], STB[:, c * 128:(c + 1) * 128],
                                start=True, stop=False), k_vtok + k_stb, [('ps', bo)], sig=False)
                            S.op('pe', lambda e: e.matmul(
                                PS[bo][:, cc * 128:(cc + 1) * 128],
                                SBf[:, c, vc * 128:(vc + 1) * 128], QD[:, c * 128:(c + 1) * 128],
                                start=False, stop=True), k_sb + k_qd, [('ps', bo)], sig=(cc == 3))
                        act(SQO[:, vc, :], PS[bo][:, :], AF.Square, [('ps', bo)], k_sqo)
                    yield
                    sbank = B_.one()
                    for vc in range(2):
                        S.op('pe', lambda e: e.matmul(
                            PS[sbank][:, :], ONESB[:, :], SQO[:, vc, :], start=(vc == 0), stop=(vc == 1)),
                            k_sqo + ['ONESB'], [('ps', sbank)], sig=True)
                    rstd_from(Rr[:, :], k_r, PS[sbank][:, :], [('ps', sbank)], 1.0 / 256.0)
                    for vc in range(2):
                        bo = obk[vc]
                        sl = slice(hf * 512, (hf + 1) * 512)
                        S.op('dve', lambda e: e.scalar_tensor_tensor(
                            out=Tt[:, vc, :], in0=PS[bo][:, :], scalar=ppc(PP_GN + vc), in1=Rr[:, :],
                            op0=ALU.mult, op1=ALU.mult), [('ps', bo), 'PP'] + k_r, k_t)
                        S.op('dve', lambda e: e.tensor_tensor(
                            out=Y[:, 2 * h + vc, sl], in0=Tt[:, vc, :], in1=SGA[:, vc, sl],
                            op=ALU.mult), k_t + k_sga, [('Y', 2 * h + vc)])
                    yield

            def conv(g, o, bl, main, first=False):
                B_ = Banks(bl)
                hk = ['halo%d' % g]
                if not main:
                    SIG, k_sig = vf(o + 0, 256)
                    Cb, k_c = vb(o + 256, 512)
                    slot = next_w(('cg', g))
                    b1 = B_.one()
                    proj(slot, (b1, b1), lambda kc, hf: U[:, kc, 896:1024], u_keys, halves=(0,), N=128)
                    sigmoid_chain(SIG[:, 0:128], k_sig, PS[b1][:, 0:128], [('ps', b1)])
                    yield
                    slot = next_w(('cv', g))
                    b2 = B_.one()
                    proj(slot, (b2, b2), lambda kc, hf: U[:, kc, 896:1024], u_keys, halves=(0,), N=128)
                    S.op('dve', lambda e: e.tensor_tensor(
                        out=Cb[:, 0:128], in0=PS[b2][:, 0:128], in1=SIG[:, 0:128], op=ALU.mult),
                        [('ps', b2)] + k_sig, k_c)
                    S.op('dve', lambda e: e.tensor_copy(out=HALO[:, g, 0:30], in_=Cb[:, 98:128]),
                         k_c, hk)
                    yield
                    return
                SIG, k_sig = vf(o + 0, 1024)
                Cb, k_c = vb(o + 1024, 1536)
                SGB, k_sgb = vb(o + 1792, 1024)
                SG2, k_sg2 = SIG, k_sig
                TAPS, k_taps = vb(o + 2304, 4096)
                TAPS = TAPS.rearrange("p (a b) -> p a b", a=32)
                XC, k_xc = vf(o + 4352, 512)
                XCB, k_xcb = vb(o + 4864, 512)
                Dd, k_d = vf(o + 5120, 512)
                SQC, k_sqc = vb(o + 5632, 512)
                RC, k_rc = vf(o + 5888, 512)
                A0, k_a0 = vf(o + 6400, 512)
                SG3, k_sg3 = vf(o + 6912, 512)
                UH = PBLK[:, :, :].rearrange("p a b -> p (a b)").rearrange("p (a b) -> p a b", a=KC)
                k_uh = [('pb', 0), ('pb', 1)]
                SIGX, k_sigx = vf(o + 4352, 128)
                slot = next_w(('cg', g))
                b1 = B_.pair()
                proj(slot, b1, u_rhs, u_keys)
                if first:
                    bx = B_.one()
                    for kc in range(KC):
                        S.op('pe', lambda e: e.matmul(PS[bx][:, 0:128], W[slot][:, kc, :], UH[:, kc, :],
                                                      start=(kc == 0), stop=(kc == KC - 1)),
                             [('w', slot)] + k_uh, [('ps', bx)], sig=(kc == KC - 1))
                    sigmoid_chain(SIGX[:, :], k_sigx, PS[bx][:, 0:128], [('ps', bx)])
                for hf in range(2):
                    sl = slice(hf * 512, (hf + 1) * 512)
                    sigmoid_chain(SIG[:, sl], k_sig, PS[b1[hf]][:, :], [('ps', b1[hf])])
                S.op('dve', lambda e: e.tensor_tensor(
                    out=TAPS[:, 0:31, :],
                    in0=IDF.unsqueeze(1).to_broadcast([128, 31, 128]),
                    in1=PP[:, PP_CW + g * 31:PP_CW + (g + 1) * 31].unsqueeze(2).to_broadcast([128, 31, 128]),
                    op=ALU.mult), ['CST', 'PP'], k_taps)
                yield
                slot = next_w(('cv', g))
                b2 = B_.pair()
                proj(slot, b2, u_rhs, u_keys)
                if first:
                    bx2 = B_.one()
                    for kc in range(KC):
                        S.op('pe', lambda e: e.matmul(PS[bx2][:, 0:128], W[slot][:, kc, :], UH[:, kc, :],
                                                      start=(kc == 0), stop=(kc == KC - 1)),
                             [('w', slot)] + k_uh, [('ps', bx2)], sig=(kc == KC - 1))
                    S.op('dve', lambda e: e.tensor_tensor(
                        out=Cb[:, 0:30], in0=PS[bx2][:, 98:128], in1=SIGX[:, 98:128], op=ALU.mult),
                        [('ps', bx2)] + k_sigx, k_c)
                else:
                    S.op('dve', lambda e: e.tensor_copy(out=Cb[:, 0:30], in_=HALO[:, g, 0:30]), hk, k_c)
                for hf in range(2):
                    sl = slice(hf * 512, (hf + 1) * 512)
                    S.op('dve', lambda e: e.tensor_tensor(
                        out=Cb[:, 30 + hf * 512:30 + (hf + 1) * 512], in0=PS[b2[hf]][:, :],
                        in1=SIG[:, sl], op=ALU.mult), [('ps', b2[hf])] + k_sig, k_c)
                S.op('dve', lambda e: e.tensor_copy(out=HALO[:, g, 0:30], in_=Cb[:, 1024:1054]), k_c, hk)
                yield
                slot = next_w(('gb', g))
                b3 = B_.pair()
                proj(slot, b3, u_rhs, u_keys)
                for hf in range(2):
                    sl = slice(hf * 512, (hf + 1) * 512)
                    sigmoid_chain(SG2[:, sl], k_sg2, PS[b3[hf]][:, :], [('ps', b3[hf])])
                    S.op('dve', lambda e: e.tensor_tensor(
                        out=SGB[:, sl], in0=PS[b3[hf]][:, :], in1=SG2[:, sl], op=ALU.mult),
                        [('ps', b3[hf])] + k_sg2, k_sgb)
                yield
                for hf in range(2):
                    sl = slice(hf * 512, (hf + 1) * 512)
                    ba = B_.one()
                    for k in range(31):
                        S.op('pe', lambda e: e.matmul(
                            PS[ba][:, :], TAPS[:, k, :], Cb[:, hf * 512 + k:hf * 512 + k + 512],
                            start=(k == 0), stop=(k == 30)), k_taps + k_c, [('ps', ba)],
                            sig=(k == 30))
                    act(XC[:, :], PS[ba][:, :], AF.Identity, [('ps', ba), 'PP'], k_xc,
                        bias=ppc(PP_CB + g))
                    act(XCB[:, :], PS[ba][:, :], AF.Identity, [('ps', ba), 'PP'], k_xcb,
                        bias=ppc(PP_CB + g))
                    yield
                    bm = B_.one()
                    S.op('pe', lambda e: e.matmul(PS[bm][:, :], ONESB[:, :], XCB[:, :],
                                                  start=True, stop=True),
                         k_xcb + ['ONESB'], [('ps', bm)])
                    S.op('dve', lambda e: e.scalar_tensor_tensor(
                        out=Dd[:, :], in0=PS[bm][:, :], scalar=-1.0 / 128.0, in1=XC[:, :],
                        op0=ALU.mult, op1=ALU.add), [('ps', bm)] + k_xc, k_d)
                    act(SQC[:, :], Dd[:, :], AF.Square, k_d, k_sqc)
                    yield
                    bv = B_.one()
                    S.op('pe', lambda e: e.matmul(PS[bv][:, :], ONESB[:, :], SQC[:, :],
                                                  start=True, stop=True),
                         k_sqc + ['ONESB'], [('ps', bv)])
                    rstd_from(RC[:, :], k_rc, PS[bv][:, :], [('ps', bv)], 1.0 / 128.0)
                    S.op('dve', lambda e: e.tensor_tensor(out=Dd[:, :], in0=Dd[:, :], in1=RC[:, :],
                                                          op=ALU.mult), k_d + k_rc, k_d)
                    act(A0[:, :], Dd[:, :], AF.Identity, k_d + ['PP'], k_a0,
                        bias=ppc(PP_LB + g), scale=ppc(PP_LG + g))
                    sigmoid_chain(SG3[:, :], k_sg3, A0[:, :], k_a0)
                    S.op('dve', lambda e: e.tensor_tensor(out=A0[:, :], in0=A0[:, :], in1=SG3[:, :],
                                                          op=ALU.mult), k_a0 + k_sg3, k_a0)
                    S.op('dve', lambda e: e.tensor_tensor(
                        out=Y[:, 8 + g, sl], in0=A0[:, :], in1=SGB[:, sl], op=ALU.mult),
                        k_a0 + k_sgb, [('Y', 8 + g)])

            XS = [vf(18432, 1024), vf(19456, 1024)]
            XSH = [vf(18432, 512), vf(18944, 512), vf(19456, 512)]
            SQN = [vb(19968, 512), vb(20224, 512)]
            PTF = PT[:, :].bitcast(F32)

            def xrows(kc):
                return slice(kc * 128, (kc + 1) * 128)

            def a1_stats(blk, bank, bkey):
                t0 = blk * TB
                R = RS[blk % 2]
                steps = [(hf, kc) for hf in range(2) for kc in range(KC)]
                n = len(steps)

                def dma_i(i):
                    hf, kc = steps[i]
                    xs, k_xs = XSH[i % 3]
                    S.dma('sp', lambda e: e.dma_start(
                        out=xs[:, :], in_=xT[xrows(kc), t0 + hf * 512:t0 + (hf + 1) * 512]),
                        [], k_xs, 'xs%d' % (i % 3))

                def sq_i(i):
                    xs, k_xs = XSH[i % 3]
                    sq, k_sq = SQN[i % 2]
                    act(sq[:, :], xs[:, :], AF.Square, k_xs, k_sq)

                def mm_i(i):
                    hf, kc = steps[i]
                    sq, k_sq = SQN[i % 2]
                    S.op('pe', lambda e: e.matmul(bank, ONESB[:, :], sq[:, :],
                                                  start=(kc == 0), stop=(kc == KC - 1)),
                         k_sq + ['ONESB'], [bkey], sig=True)
                    if kc == KC - 1:
                        rstd_from(R[:, hf * 512:(hf + 1) * 512], [('rstd', blk % 2, hf)], bank, [bkey],
                                  1.0 / D)

                dma_i(0)
                dma_i(1)
                for i in range(n + 1):
                    if i + 2 < n:
                        dma_i(i + 2)
                    if i < n:
                        sq_i(i)
                    if i >= 1:
                        mm_i(i - 1)
                    if i % 2 == 1:
                        yield

            def a1_norm(blk):
                t0 = blk * TB
                R = RS[blk % 2]

                def dma_k(kc):
                    xs, k_xs = XS[kc % 2]
                    S.dma('sp', lambda e: e.dma_start(out=xs[:, :], in_=xT[xrows(kc), t0:t0 + TB]),
                          [], k_xs, 'xn%d' % (kc % 2))

                dma_k(0)
                for kc in range(KC):
                    if kc + 1 < KC:
                        dma_k(kc + 1)
                    xs, k_xs = XS[kc % 2]
                    S.op('dve', lambda e: e.scalar_tensor_tensor(
                        out=U[:, kc, :], in0=xs[:, :], scalar=ppc(PP_NM + kc), in1=R[:, :],
                        op0=ALU.mult, op1=ALU.mult),
                        k_xs + [('rstd', blk % 2, 0), ('rstd', blk % 2, 1), 'PP'], [('U', kc)])
                    if kc % 2 == 1:
                        yield

            def a1_full(blk):
                t0 = blk * TB
                R = RS[blk % 2]
                rkk = [('rstd', blk % 2, 0), ('rstd', blk % 2, 1)]
                Xv = BIG[:, 0:KC * TB].rearrange("p (a b) -> p a b", a=KC)

                def xk(kc):
                    return bk(kc * TB, (kc + 1) * TB)
                for kc in range(KC):
                    S.dma('sp', lambda e: e.dma_start(out=Xv[:, kc, :], in_=xT[xrows(kc), t0:t0 + TB]),
                          [], xk(kc), 'x%d' % kc)
                for kc in range(KC):
                    sq = SQ[:, kc % 2, :]
                    if kc % 2 == 0:
                        act(sq, Xv[:, kc, :], AF.Square, xk(kc), [('sq', kc % 2)])
                    else:
                        S.op('dve', lambda e: e.tensor_tensor(out=sq, in0=Xv[:, kc, :], in1=Xv[:, kc, :],
                                                              op=ALU.mult), xk(kc), [('sq', kc % 2)])
                    for hf in range(2):
                        S.op('pe', lambda e: e.matmul(
                            PS[5 + hf][:, :], ONESB[:, :], sq[:, hf * 512:(hf + 1) * 512],
                            start=(kc == 0), stop=(kc == KC - 1)),
                            [('sq', kc % 2), 'ONESB'], [('ps', 5 + hf)], sig=True)
                for hf in range(2):
                    rstd_from(R[:, hf * 512:(hf + 1) * 512], [rkk[hf]], PS[5 + hf][:, :],
                              [('ps', 5 + hf)], 1.0 / D)
                for kc in range(KC):
                    S.op('dve', lambda e: e.scalar_tensor_tensor(
                        out=U[:, kc, :], in0=Xv[:, kc, :], scalar=ppc(PP_NM + kc), in1=R[:, :],
                        op0=ALU.mult, op1=ALU.mult),
                        xk(kc) + rkk + ['PP'], [('U', kc)])

            a1_full(0)
            if not dry:
                for _ in range(NSLOT):
                    issue_w(extra_reads=bk(15 * TB, 16 * TB))
            S.dma('pool', lambda e: e.dma_start(out=WPLE[:].rearrange("p a b -> p (a b)"), in_=wple),
                  [], ['WPLE'], 'c4')

            for blk in range(NBLK):
                main = blk >= NPRE
                t0 = blk * TB
                RSTD = RS[blk % 2]
                rk = [('rstd', blk % 2, 0), ('rstd', blk % 2, 1)]
                ALOW = RSTD[:, 0:512].bitcast(BF16)
                k_alow = ['alow'] + rk
                cur['ALOW'] = ALOW
                cur['k_alow'] = k_alow
                Hv = BIG[:, 0:KC * TB].rearrange("p (a b) -> p a b", a=KC)

                def xkeys(kc):
                    return bk(kc * TB, (kc + 1) * TB)

                proj(None, (0, 1), u_rhs, u_keys, wap=lambda kc: WALOW[:, kc, :])
                for hf in range(2):
                    sl = slice(hf * 512, (hf + 1) * 512)
                    act(ALOW[0:80, sl], PS[hf][0:80, :], AF.Copy, [('ps', hf)], k_alow)
                    S.op('dve', lambda e: e.tensor_tensor(out=ALOW[32:48, sl], in0=PS[hf][32:48, :],
                                                          in1=ALOW[32:48, sl], op=ALU.subtract),
                         [('ps', hf)] + k_alow, k_alow)

                if main:
                    lane1 = chain(*[gla(h, 0, [0, 1, 2, 3], True, 0) for h in range(4)])
                    fst = (blk == NPRE)
                    lane2 = chain(*[conv(g, 12544, [4, 5, 6], True, fst) for g in range(6)])
                    run_lanes([lane1, lane2])
                    run_lanes([conv(6, 0, [0, 1, 2, 3], True, fst), conv(7, 12544, [4, 5, 6], True, fst)])
                else:
                    tasksA = [gla(0, 0, [0, 1, 2, 3], False, 0), gla(2, 0, [0, 1, 2, 3], False, 0)]
                    tasksB = [gla(1, 7936, [4, 5, 6], False, 1), gla(3, 7936, [4, 5, 6], False, 1)]
                    run_lanes([chain(*tasksA), chain(*tasksB)])
                    if blk == NPRE - 1:
                        UHs = PBLK[:, :, :].rearrange("p a b -> p (a b)").rearrange("p (a b) -> p a b", a=KC)
                        S.op('act', lambda e: e.copy(out=UHs, in_=U[:, :, 896:1024]),
                             [('U', kc) for kc in range(KC)], [('pb', 0), ('pb', 1)])
                    a1_full(blk + 1)
                    continue

                tm = (blk - NPRE) * TB
                for kc2 in range(2):
                    S.dma('pool', lambda e: e.dma_start(
                        out=PBLK[:, kc2, :], in_=pT[kc2 * 128:(kc2 + 1) * 128, tm:tm + TB]),
                        [], [('pb', kc2)], 'p%d' % kc2)
                for oc in range(KC):
                    S.dma('sp', lambda e: e.dma_start(out=Hv[:, oc, :],
                                                      in_=xT[oc * 128:(oc + 1) * 128, t0:t0 + TB]),
                          [], xkeys(oc), 'x%d' % oc)

                cnts = {'n': 0}

                def stats_sq(oc):
                    sq = SQ[:, oc % 2, :]
                    act(sq, Hv[:, oc, :], AF.Square, xkeys(oc), [('sq', oc % 2)])

                def stats_mm(oc, do_sq=True):
                    cnts['n'] += 1
                    first = cnts['n'] == 1
                    last = cnts['n'] == KC
                    sq = SQ[:, oc % 2, :]
                    if do_sq:
                        stats_sq(oc)
                    for hf in range(2):
                        S.op('pe', lambda e: e.matmul(
                            PS[5 + hf][:, :], ONESB[:, :], sq[:, hf * 512:(hf + 1) * 512],
                            start=first, stop=last),
                            [('sq', oc % 2), 'ONESB'], [('ps', 5 + hf)], sig=True)

                def b1(ocs, banks):
                    pend = None
                    for oc in ocs:
                        slot = next_w(('wo', oc))
                        proj(slot, banks, y_rhs, y_keys)
                        if pend is not None:
                            stats_mm(pend)
                        yield
                        for hf in range(2):
                            sl = slice(hf * 512, (hf + 1) * 512)
                            S.op('dve', lambda e: e.tensor_tensor(
                                out=Hv[:, oc, sl], in0=Hv[:, oc, sl], in1=PS[banks[hf]][:, :], op=ALU.add),
                                xkeys(oc) + [('ps', banks[hf])], xkeys(oc))
                        pend = oc
                    yield
                    stats_mm(pend)

                cnts['n'] = 0
                lanesB = [b1(range(0, KC, 2), (0, 1)), b1(range(1, KC, 2), (2, 3))]
                nxt = blk + 1 < NBLK
                if nxt:
                    lanesB.append(a1_stats(blk + 1, PTF, 'pt'))
                run_lanes(lanesB)
                for hf in range(2):
                    rstd_from(RSTD[:, hf * 512:(hf + 1) * 512], [rk[hf]], PS[5 + hf][:, :],
                              [('ps', 5 + hf)], 1.0 / D)
                for kc in range(KC):
                    S.op('dve', lambda e: e.scalar_tensor_tensor(
                        out=Y[:, kc, :], in0=Hv[:, kc, :], scalar=ppc(PP_PN + kc), in1=RSTD[:, :],
                        op0=ALU.mult, op1=ALU.mult),
                        xkeys(kc) + rk + ['PP'], [('Y', kc)])

                def b2(ocs, banks, o, pbank):
                    G0, k_g0 = vf(o, 1024)
                    pend = None
                    for oc in ocs:
                        if pend is not None:
                            stats_sq(pend)
                        slot = next_w(('wp', oc))
                        proj(slot, banks, y_rhs, y_keys)
                        if pend is not None:
                            stats_mm(pend, do_sq=False)
                        pbs = []
                        for hf in range(2):
                            pb_ap = PS[4][:, :] if hf == 0 else PTF
                            pb_k = ('ps', 4) if hf == 0 else 'pt'
                            pbs.append((pb_ap, pb_k))
                            for kc2 in range(2):
                                S.op('pe', lambda e: e.matmul(
                                    pb_ap, WPLE[:, kc2, oc * 128:(oc + 1) * 128],
                                    PBLK[:, kc2, hf * 512:(hf + 1) * 512], start=(kc2 == 0), stop=(kc2 == 1)),
                                    ['WPLE', ('pb', kc2)], [pb_k], sig=(kc2 == 1))
                        for hf in range(2):
                            sl = slice(hf * 512, (hf + 1) * 512)
                            sigmoid_chain(G0[:, sl], k_g0, PS[banks[hf]][:, :], [('ps', banks[hf]), 'PN'],
                                          scale=-1.0, bias=pnc(PN_BPG + oc))
                        for hf in range(2):
                            sl = slice(hf * 512, (hf + 1) * 512)
                            pb_ap, pb_k = pbs[hf]
                            S.op('dve', lambda e: e.tensor_tensor(
                                out=G0[:, sl], in0=G0[:, sl], in1=pb_ap, op=ALU.mult),
                                k_g0 + [pb_k], k_g0)
                            S.op('dve', lambda e: e.tensor_tensor(
                                out=Hv[:, oc, sl], in0=Hv[:, oc, sl], in1=G0[:, sl], op=ALU.add),
                                xkeys(oc) + k_g0, xkeys(oc))
                        pend = oc
                        yield
                    stats_mm(pend)

                cnts['n'] = 0
                lanesB = [b2(range(0, KC, 2), (0, 1), 16384, 4), b2(range(1, KC, 2), (2, 3), 17408, 4)]
                if nxt:
                    lanesB.append(a1_norm(blk + 1))
                run_lanes(lanesB)
                for hf in range(2):
                    rstd_from(RSTD[:, hf * 512:(hf + 1) * 512], [rk[hf]], PS[5 + hf][:, :],
                              [('ps', 5 + hf)], 1.0 / D)
                for oc in range(KC):
                    S.op('dve', lambda e: e.scalar_tensor_tensor(
                        out=Hv[:, oc, :], in0=Hv[:, oc, :], scalar=ppc(PP_FN + oc), in1=RSTD[:, :],
                        op0=ALU.mult, op1=ALU.mult),
                        xkeys(oc) + rk + ['PP'], xkeys(oc))
                    S.dma('sp', lambda e: e.dma_start(
                        out=outT[oc * 128:(oc + 1) * 128, tm:tm + TB], in_=Hv[:, oc, :]),
                        xkeys(oc), [], 'o%d' % oc)
            if not dry:
                assert state['next'] == len(stream), (state['next'], len(stream))

        emit(DrySched(), True)
        emit(real, False)
        real.finish('sp')
        build_nc.last_sched = real
    return nc


def _pack_groups(w_in, w_out, w_pg):
    wg = np.empty((NG, 128, KC * 128), np.float32)

    def put(gid, cols):
        wg[gid] = cols.reshape(KC, 128, 128).transpose(1, 0, 2).reshape(128, KC * 128)
    for h in range(4):
        put(GID[('q', h)], w_in[:, h * 128:(h + 1) * 128])
        put(GID[('k', h)], w_in[:, 512 + h * 128:512 + (h + 1) * 128])
        for vc in range(2):
            c0 = 1024 + h * 256 + vc * 128
            put(GID[('v', h, vc)], w_in[:, c0:c0 + 128])
            c0 = 2064 + h * 256 + vc * 128
            put(GID[('ga', h, vc)], w_in[:, c0:c0 + 128])
    for g in range(8):
        put(GID[('cv', g)], w_in[:, 3088 + g * 128:3088 + (g + 1) * 128])
        put(GID[('cg', g)], w_in[:, 4112 + g * 128:4112 + (g + 1) * 128])
        put(GID[('gb', g)], w_in[:, 5136 + g * 128:5136 + (g + 1) * 128])
    for oc in range(16):
        put(GID[('wo', oc)], w_out[:, oc * 128:(oc + 1) * 128])
        put(GID[('wp', oc)], w_pg[:, oc * 128:(oc + 1) * 128])
    return wg


def _vec16(v):
    return np.ascontiguousarray(v.reshape(-1, 128).T)


def _prep_inputs(x, p, norm_mix, w_in, w_alpha, b_alpha, gla_norm, conv_w, conv_b,
                 conv_ln_g, conv_ln_b, w_out, ple_norm, w_ple_gate, b_ple_gate, w_ple,
                 final_norm):
    f = lambda a: np.asarray(a, dtype=np.float32)
    x = f(x); p = f(p)[0]
    w_in = f(w_in)[0]; w_out = f(w_out)[0]; w_pg = f(w_ple_gate)[0]; w_ple = f(w_ple)[0]
    wg = _pack_groups(w_in, w_out, w_pg)
    wal = np.zeros((D, 128), np.float32)
    for r in range(3):
        wal[:, 32 * r:32 * r + 16] = w_in[:, 2048:2064]
    walow = np.ascontiguousarray(wal.reshape(KC, 128, 128).transpose(1, 0, 2).reshape(128, KC * 128))
    wple = np.ascontiguousarray(w_ple.reshape(2, 128, D).transpose(1, 0, 2).reshape(128, 2 * D))
    pp = np.zeros((128, NPP), np.float32)
    pp[:, PP_NM:PP_NM + 16] = _vec16(f(norm_mix)[0])
    pp[:, PP_PN:PP_PN + 16] = _vec16(f(ple_norm)[0])
    pp[:, PP_FN:PP_FN + 16] = _vec16(f(final_norm))
    pp[:, PP_BPG:PP_BPG + 16] = _vec16(f(b_ple_gate)[0])
    pp[:, PP_BA:PP_BA + 4] = _vec16(f(b_alpha)[0])
    pp[:, PP_GN:PP_GN + 2] = _vec16(f(gla_norm)[0])
    pp[:, PP_CB:PP_CB + 8] = _vec16(f(conv_b)[0])
    pp[:, PP_LG:PP_LG + 8] = _vec16(f(conv_ln_g)[0])
    pp[:, PP_LB:PP_LB + 8] = _vec16(f(conv_ln_b)[0])
    cw = f(conv_w)[0]
    pp[:, PP_CW:PP_CW + 8 * 31] = cw.reshape(31, 8, 128).transpose(2, 1, 0).reshape(128, 8 * 31)
    walpha = np.ascontiguousarray(f(w_alpha)[0])
    cst = np.zeros((128, 256), np.float32)
    cst[:, 0:128] = np.eye(128, dtype=np.float32)
    cst[:, 128:256] = np.triu(np.ones((128, 128), np.float32))
    shared = {"wg": wg, "walow": walow, "wple": wple, "pp": pp, "walpha": walpha, "cst": cst}
    in_maps = []
    for b in range(NB):
        xTb = np.ascontiguousarray(x[b].T)
        pTb = np.ascontiguousarray(p[b].T)
        for hf in range(2):
            if hf == 0:
                xc = np.concatenate([np.zeros((D, 2048), np.float32), xTb[:, 0:2048]], axis=1)
            else:
                xc = xTb
            m = dict(shared)
            m["xT"] = np.ascontiguousarray(xc)
            m["pT"] = np.ascontiguousarray(pTb[:, hf * 2048:(hf + 1) * 2048])
            in_maps.append(m)
    return in_maps


def kernel(**inputs):
    in_maps = _prep_inputs(**inputs)
    nc = build_nc()
    res = run_bass_kernel_spmd(nc, in_maps, core_ids=list(range(8)))
    out = np.empty((NB, SEQ, D), np.float32)
    for b in range(NB):
        for hf in range(2):
            r = res.results[2 * b + hf]["outT"]
            out[b, hf * 2048:(hf + 1) * 2048, :] = r.T
    return out
```
